# Optimizing a Trainium2 kernel written in Bass

```python
import jax, jax.numpy as jnp
from jax import lax
import numpy as np

D_MODEL = 1024
BATCH = 4
SEQ = 4096
DEPTH = 1

CHUNK = 64
GMLP_BLOCK = 128
A_WIDTH = D_MODEL
A_GROUPS = 8
A_GROUP_DIM = A_WIDTH // A_GROUPS
POOL_WINDOWS = (2, 4, 8, 16)
B_WIDTH = D_MODEL
B_GROUPS = len(POOL_WINDOWS)
B_GROUP_DIM = B_WIDTH // B_GROUPS
IN_WIDTH = 2 * A_WIDTH + B_WIDTH
D_FF = ((8 * D_MODEL // 3) + 127) // 128 * 128
CONV_WIDTH = 3
N_ADA = 6
EPS = 1e-6

kernel_name = "chunk_causal_gmlp_pool_hybrid_adaln"


def rmsnorm(x, g):
    xf = x.astype(jnp.float32)
    y = xf * lax.rsqrt(jnp.mean(xf * xf, axis=-1, keepdims=True) + EPS)
    return (y * g.astype(jnp.float32)).astype(x.dtype)


def layernorm(x, g, b):
    xf = x.astype(jnp.float32)
    mu = jnp.mean(xf, axis=-1, keepdims=True)
    var = jnp.mean(jnp.square(xf - mu), axis=-1, keepdims=True)
    y = (xf - mu) * lax.rsqrt(var + EPS)
    return (y * g.astype(jnp.float32) + b.astype(jnp.float32)).astype(x.dtype)


def chunk_causal_block_mask():
    p = jnp.arange(GMLP_BLOCK)
    return (p[None, :] // CHUNK) <= (p[:, None] // CHUNK)


def spatial_gating(v, w_s, b_s):
    bsz, s_len, _ = v.shape
    vb = v.reshape(bsz, s_len // GMLP_BLOCK, GMLP_BLOCK, A_GROUPS, A_GROUP_DIM)
    w = jnp.where(chunk_causal_block_mask()[None], w_s, jnp.zeros((), w_s.dtype))
    s = jnp.einsum('gpq,bnqgd->bnpgd', w, vb) + b_s.T[None, None, :, :, None]
    return s.reshape(bsz, s_len, A_WIDTH)


def multiscale_pool(hb, w_pool, b_pool, scale):
    bsz, s_len, _ = hb.shape
    hf = hb.astype(jnp.float32).reshape(bsz, s_len, B_GROUPS, B_GROUP_DIM)
    cs = jnp.cumsum(hf, axis=1)
    t = jnp.arange(s_len)
    outs = []
    for gi, w in enumerate(POOL_WINDOWS):
        csg = cs[:, :, gi]
        lo = jnp.pad(csg[:, :s_len - w], ((0, 0), (w, 0), (0, 0)))
        count = jnp.minimum(t + 1, w).astype(jnp.float32)
        mean = (csg - lo) / count[None, :, None]
        outs.append(mean - hf[:, :, gi])
    pooled = jnp.stack(outs, axis=2)
    mixed = jnp.einsum('bsgc,gcd->bsgd', pooled, w_pool.astype(jnp.float32)) + b_pool.astype(jnp.float32)
    y = mixed.reshape(bsz, s_len, B_WIDTH) * scale.astype(jnp.float32)
    return y.astype(hb.dtype)


def causal_dwconv(z, w, b):
    s_len = z.shape[1]
    zp = jnp.pad(z, ((0, 0), (CONV_WIDTH - 1, 0), (0, 0)))
    out = b
    for k in range(CONV_WIDTH):
        out = out + zp[:, k:k + s_len] * w[k]
    return out


def setup_inputs(seed: int = 0) -> dict:
    key = jax.random.key(seed)
    ks = jax.random.split(key, 24)
    L, D = DEPTH, D_MODEL
    f32 = jnp.float32

    def nrm(k, shape, s):
        return jax.random.normal(k, shape, f32) * s

    return {
        "x": jax.random.normal(ks[0], (BATCH, SEQ, D), f32),
        "c": jax.random.normal(ks[1], (BATCH, D), f32),
        "w_ada": nrm(ks[2], (L, D, N_ADA * D), 0.5 * D ** -0.5),
        "b_ada": nrm(ks[3], (L, N_ADA * D), 0.02),
        "g_norm1": 1.0 + nrm(ks[4], (L, D), 0.05),
        "w_in": nrm(ks[5], (L, D, IN_WIDTH), D ** -0.5),
        "ln_v_g": 1.0 + nrm(ks[6], (L, A_WIDTH), 0.05),
        "ln_v_b": nrm(ks[7], (L, A_WIDTH), 0.02),
        "w_spatial": nrm(ks[8], (L, A_GROUPS, GMLP_BLOCK, GMLP_BLOCK), GMLP_BLOCK ** -0.5),
        "b_spatial": 1.0 + nrm(ks[9], (L, A_GROUPS, GMLP_BLOCK), 0.05),
        "w_pool": nrm(ks[10], (L, B_GROUPS, B_GROUP_DIM, B_GROUP_DIM), B_GROUP_DIM ** -0.5),
        "b_pool": nrm(ks[11], (L, B_GROUPS, B_GROUP_DIM), 0.02),
        "pool_scale": 1.0 + nrm(ks[12], (L, B_WIDTH), 0.05),
        "w_proj_a": nrm(ks[13], (L, A_WIDTH, D), A_WIDTH ** -0.5),
        "w_proj_b": nrm(ks[14], (L, B_WIDTH, D), B_WIDTH ** -0.5),
        "w_gate": nrm(ks[15], (L, D, 2 * D), D ** -0.5),
        "b_gate": nrm(ks[16], (L, 2 * D), 0.02),
        "w_out": nrm(ks[17], (L, D, D), D ** -0.5),
        "g_norm2": 1.0 + nrm(ks[18], (L, D), 0.05),
        "w_up": nrm(ks[19], (L, D, 2 * D_FF), D ** -0.5),
        "conv_w": nrm(ks[20], (L, CONV_WIDTH, 2 * D_FF), CONV_WIDTH ** -0.5),
        "conv_b": nrm(ks[21], (L, 2 * D_FF), 0.02),
        "w_down": nrm(ks[22], (L, D_FF, D), D_FF ** -0.5),
        "g_final": 1.0 + nrm(ks[23], (D,), 0.05),
    }


def reference(x, c, w_ada, b_ada, g_norm1, w_in, ln_v_g, ln_v_b, w_spatial, b_spatial,
              w_pool, b_pool, pool_scale, w_proj_a, w_proj_b, w_gate, b_gate, w_out,
              g_norm2, w_up, conv_w, conv_b, w_down, g_final):
    for l in range(DEPTH):
        mod = jax.nn.silu(c) @ w_ada[l] + b_ada[l]
        sh1, sc1, gt1, sh2, sc2, gt2 = [m[:, None, :] for m in jnp.split(mod, N_ADA, axis=-1)]

        h = rmsnorm(x, g_norm1[l]) * (1.0 + sc1) + sh1
        z = h @ w_in[l]
        za = jax.nn.gelu(z[..., :2 * A_WIDTH], approximate=False)
        u, v = za[..., :A_WIDTH], za[..., A_WIDTH:]
        v = layernorm(v, ln_v_g[l], ln_v_b[l])
        y_a = u * spatial_gating(v, w_spatial[l], b_spatial[l])
        y_b = multiscale_pool(z[..., 2 * A_WIDTH:], w_pool[l], b_pool[l], pool_scale[l])
        gates = jax.nn.sigmoid(h @ w_gate[l] + b_gate[l])
        g_a, g_b = gates[..., :D_MODEL], gates[..., D_MODEL:]
        merged = g_a * (y_a @ w_proj_a[l]) + g_b * (y_b @ w_proj_b[l])
        x = x + gt1 * (merged @ w_out[l])

        h2 = rmsnorm(x, g_norm2[l]) * (1.0 + sc2) + sh2
        up = causal_dwconv(h2 @ w_up[l], conv_w[l], conv_b[l])
        f = jax.nn.silu(up[..., :D_FF]) * up[..., D_FF:]
        x = x + gt2 * (f @ w_down[l])
    return rmsnorm(x, g_final)
```

```python
import contextlib
import numpy as np
import concourse.bass as bass
import concourse.mybir as mybir
from concourse.bass_utils import run_bass_kernel_spmd

F32 = mybir.dt.float32
BF16 = mybir.dt.bfloat16
AF = mybir.ActivationFunctionType
ALU = mybir.AluOpType

D = 1024
SEQ = 4096
BATCH = 4
NCORE = 8
TOK = 2048
NBLK = TOK // 128
DFF = 2816
NCH_UP = 44
EPS = 1e-6
WINS = (2, 4, 8, 16)
NSLOT = 5
ENGS = ("pe", "act", "dve", "pool", "sp")

SPACE = {"arena": 0, "ps": 1 << 24, "wscr": 1 << 28}


def _esize(dt):
    return 2 if dt == BF16 else 4


def ap_ivs(ap):
    name = ap.tensor.name
    if name not in SPACE:
        return []
    base = SPACE[name]
    es = _esize(ap.dtype)
    dims = list(ap.ap)
    if name == "wscr":
        off = ap.offset
    else:
        pstride = dims[0][0]
        off = ap.offset % pstride if pstride > 0 else ap.offset
        dims = dims[1:]

    def rec(dims):
        if not dims:
            return [(0, 1)]
        (s, n) = dims[0]
        inner = rec(dims[1:])
        ext = max(h for _, h in inner)
        lo0 = min(l for l, _ in inner)
        if n == 1:
            return inner
        if s <= 0:
            return [(lo0, ext)]
        if s < ext or n > 64 or len(inner) > 1:
            return [(lo0, ext + (n - 1) * s)]
        if s == ext and lo0 == 0:
            return [(0, ext + (n - 1) * s)]
        return [(i * s + lo0, i * s + ext) for i in range(n)]

    ivs = [(base + (off + l) * es, base + (off + h) * es) for (l, h) in rec(dims)]
    if name == "ps":
        ivs = sorted(set((base + ((lo - base) // 2048) * 2048, base + ((hi - 1 - base) // 2048 + 1) * 2048) for lo, hi in ivs))
    return ivs


class Op:
    __slots__ = ("eng", "fn", "deps", "signal", "sem", "val", "dkey", "idx", "clock", "waits")

    def __init__(self, eng, fn, dkey):
        self.eng = eng
        self.fn = fn
        self.deps = []
        self.signal = False
        self.sem = None
        self.val = 0
        self.dkey = dkey
        self.clock = None
        self.waits = []


class Prog:
    def __init__(self, nc):
        self.nc = nc
        self.ops = []
        self.W = []
        self.R = []
        self.finals = []
        self.barrier_keys = set()

    def op(self, eng, fn, reads=(), writes=(), dkey=None):
        o = Op(eng, fn, dkey)
        o.idx = len(self.ops)
        riv = []
        for a in reads:
            riv.extend(ap_ivs(a))
        wiv = []
        for a in writes:
            wiv.extend(ap_ivs(a))
        deps = set()
        W, R = self.W, self.R
        PS0 = SPACE["ps"]
        PS1 = PS0 + (1 << 20)
        for (lo, hi) in riv:
            for (l, h, q) in W:
                if l < hi and lo < h:
                    deps.add(q)
            if PS0 <= lo < PS1:
                for (l, h, q) in R:
                    if l < hi and lo < h and q.eng != eng:
                        deps.add(q)
        for (lo, hi) in wiv:
            for (l, h, q) in W:
                if l < hi and lo < h:
                    deps.add(q)
            for (l, h, q) in R:
                if l < hi and lo < h:
                    deps.add(q)
        for (lo, hi) in wiv:
            W = [t for t in W if not (lo <= t[0] and t[1] <= hi)]
            R = [t for t in R if not (lo <= t[0] and t[1] <= hi)]
            W.append((lo, hi, o))
        is_cmp = dkey is None
        for (lo, hi) in riv:
            if is_cmp:
                R = [t for t in R if not (t[2].eng == eng and t[2].dkey is None and lo <= t[0] and t[1] <= hi)]
            R.append((lo, hi, o))
        self.W, self.R = W, R
        for d in deps:
            if d is o:
                continue
            if d.eng == "pe" and eng == "pe" and d.dkey is None and dkey is None:
                continue
            o.deps.append(d)
            d.signal = True
        if dkey is not None:
            o.signal = True
        self.ops.append(o)
        return o

    def finish(self, ops):
        for o in ops:
            o.signal = True
            self.finals.append(o)

    def build(self):
        nc = self.nc
        st = contextlib.ExitStack()
        eng_sem, eng_cnt = {}, {e: 0 for e in ENGS}
        dma_sem, dma_cnt = {}, {}
        for o in self.ops:
            if not o.signal:
                continue
            if o.dkey is not None:
                if o.dkey not in dma_sem:
                    dma_sem[o.dkey] = st.enter_context(nc.semaphore("d_" + str(o.dkey)))
                    dma_cnt[o.dkey] = 0
                dma_cnt[o.dkey] += 16
                o.sem, o.val = dma_sem[o.dkey], dma_cnt[o.dkey]
            else:
                if o.eng not in eng_sem:
                    eng_sem[o.eng] = st.enter_context(nc.semaphore("s_" + o.eng))
                eng_cnt[o.eng] += 1
                o.sem, o.val = eng_sem[o.eng], eng_cnt[o.eng]
        for o in self.ops:
            if o.signal and o.dkey in self.barrier_keys:
                o.val = dma_cnt[o.dkey]
        clock = {e: {} for e in ENGS}
        nwait = 0
        for o in self.ops:
            ck = clock[o.eng]
            for d in sorted(o.deps, key=lambda d: -d.idx):
                key = d.sem.num
                if ck.get(key, 0) >= d.val:
                    continue
                o.waits.append((d.sem, d.val))
                nwait += 1
                ck[key] = d.val
                if d.clock is not None:
                    for k2, v2 in d.clock.items():
                        if ck.get(k2, 0) < v2:
                            ck[k2] = v2
            if o.signal:
                o.clock = dict(ck)
        self.nwait = nwait
        by_eng = {e: [] for e in ENGS}
        for o in self.ops:
            by_eng[o.eng].append(o)
        finals = self.finals

        def emit(engobj, lst, last=False):
            for o in lst:
                for (s, v) in o.waits:
                    engobj.wait_ge(s, v)
                ins = o.fn(engobj)
                if o.signal:
                    ins.then_inc(o.sem, 16 if o.dkey is not None else 1)
            if last:
                for o in finals:
                    engobj.wait_ge(o.sem, o.val)

        with nc.Block() as block:
            @block.tensor
            def _(e):
                emit(e, by_eng["pe"])

            @block.scalar
            def _(e):
                emit(e, by_eng["act"])

            @block.vector
            def _(e):
                emit(e, by_eng["dve"])

            @block.gpsimd
            def _(e):
                emit(e, by_eng["pool"])

            @block.sync
            def _(e):
                emit(e, by_eng["sp"], last=True)
        st.close()


U_WIN, U_GATE, U_PB, U_PA, U_OUT, U_UP, U_DN = 0, 6, 10, 12, 14, 16, 27
NUNIT = 33


def build_program(n_tiles=5, debug_taps=()):
    nc = bass.Bass("TRN2", target_bir_lowering=False)
    P = Prog(nc)

    def din(name, shape):
        return nc.dram_tensor(name, list(shape), F32, kind="ExternalInput").ap()

    x_d = din("x", [NBLK + 1, 128, D])
    c_d = din("c", [128, 8])
    cmask_d = din("cmask", [128, 1])
    invc_d = din("invc", [128, 64])
    ident_d = din("ident", [128, 128])
    wada_d = din("w_ada", [12, 128, 8, 512])
    bada_d = din("b_ada", [1, 6 * D])
    g1_d = din("g_norm1", [1, D])
    g2_d = din("g_norm2", [1, D])
    wun_d = din("w_units", [U_DN, 128, 8, 512])
    wdn_d = din("w_down", [2, 128, 22, 512])
    lng_d = din("ln_v_g", [D])
    lnb_d = din("ln_v_b", [D])
    gfin_d = din("g_final", [D])
    wsp_d = din("w_spT", [128, 8, 128])
    bsp_d = din("b_sp", [1, D])
    wpool_d = din("w_pool", [128, 4, 2, 256])
    bpool_d = din("b_pool", [128, 8])
    pscale_d = din("pool_scale", [128, 8])
    bgate_d = din("b_gate", [128, 16])
    convw_d = din("conv_w", [128, NCH_UP, 3])
    convb_d = din("conv_b", [128, NCH_UP])
    out_d = nc.dram_tensor("out", [NBLK, 128, D], F32, kind="ExternalOutput").ap()
    wscr = nc.dram_tensor("wscr", [NUNIT, 128, 4096], BF16, kind="Internal").ap()

    ARENA_BYTES = 207 * 1024
    arena = nc.alloc_sbuf_tensor("arena", [128, ARENA_BYTES // 4], F32)
    ps = nc.alloc_psum_tensor("ps", [128, 8, 512], F32)

    def view(off, nbytes, dt=F32, shape=None, rows=None):
        assert off % 4 == 0 and nbytes % 4 == 0 and off + nbytes <= ARENA_BYTES, (off, nbytes)
        a = arena[:, off // 4:(off + nbytes) // 4] if rows is None else arena[0:rows, off // 4:(off + nbytes) // 4]
        if dt == BF16:
            a = a.bitcast(BF16)
        if shape is not None and len(shape) == 2:
            a = a.rearrange("p (a b) -> p a b", a=shape[0])
        elif shape is not None and len(shape) == 3:
            a = a.rearrange("p (a b c) -> p a b c", a=shape[0], b=shape[1])
        return a

    cur = [0]

    def alloc(nbytes, dt=F32, shape=None, rows=None):
        v = view(cur[0], nbytes, dt, shape, rows)
        cur[0] += (nbytes + 31) // 32 * 32
        return v

    ident = alloc(512)
    wmT = alloc(2048, BF16, (8, 128))
    wpool = alloc(4096, BF16, (4, 2, 256))
    lng_b = alloc(4096)
    lnb_b = alloc(4096)
    gfin_b = alloc(4096)
    gt1h_b = alloc(4096)
    gt2_b = alloc(4096)
    modcols = alloc(128)
    bpool = alloc(32)
    pscale = alloc(32)
    hbgate = alloc(64)
    convw = alloc(NCH_UP * 12, F32, (NCH_UP, 3))
    convb = alloc(NCH_UP * 4)
    invc = alloc(256, F32, (4, 16))
    cmask = alloc(32)
    mh = alloc(32)
    ssq = [alloc(32) for _ in range(3)]
    msq = [alloc(32) for _ in range(3)]
    rstd = [alloc(32) for _ in range(3)]
    bnst = alloc(4 * 2 * 6 * 4, F32, (4, 2, 6))
    bnmv = alloc(4 * 2 * 4, F32, (4, 2))
    lnms = alloc(32)
    lnrs = alloc(32)
    lnnm = alloc(32)
    Hh = alloc(NCH_UP * 8, F32, (NCH_UP, 2))
    ph = alloc(NCH_UP * 8, F32, (NCH_UP, 2))
    htmp = alloc(NCH_UP * 4)
    ph0 = alloc(NCH_UP * 8, F32, (NCH_UP, 2))
    hh2 = alloc(32, BF16, (8, 2))
    zpre = alloc(8 * 16 * 4, F32, (8, 16))
    ones_bf = alloc(256, BF16, rows=33)
    ones_f = alloc(512, F32, rows=1)
    bs2 = alloc(2048, BF16, rows=33)
    hT_off = cur[0]
    hT = alloc(8192, BF16, (8, 512))
    bs_f = view(hT_off, 4096, F32, rows=33)
    bs_t = view(hT_off + 4096, 2048, BF16, rows=33)
    ring = [alloc(8192, BF16, (8, 512)) for _ in range(NSLOT)]
    f_off = cur[0]
    f_b = alloc(22528, BF16, (22, 512))
    xb = [alloc(16384, F32, (4, D)) for _ in range(2)]
    base_mix = cur[0]
    RX = base_mix
    zb = view(RX, 16896, F32, (8, 528))
    ptmp_d = [view(RX + 16896, 2112), view(RX + 19008, 2112)]
    pooled = view(RX + 21120, 8192, BF16, (8, 512))
    ya = view(RX, 8192, BF16, (8, 512))
    t2 = view(RX + 8192, 16384, F32, (8, 512))
    t1 = [view(RX + 24576, 2048), view(RX + 26624, 2048)]
    RY = RX + 29312
    vge = [view(RY, 4096), view(RY + 4096, 4096)]
    vg = view(RY + 8192, 8192, BF16, (4, D))
    u_b = view(RY + 16384, 8192, BF16, (8, 512))
    xn = view(RY, 16384, F32, (4, D))
    merged = view(RY + 16384, 8192, BF16, (8, 512))
    RZ = RY + 24576
    yb = view(RZ, 8192, BF16, (8, 512))
    ptmp_p = [view(RZ, 2112), view(RZ + 2112, 2112)]
    th = view(RZ + 8192, 8192, BF16, (8, 512))
    Abuf = [view(RX, 8192, F32, (4, 512)), view(RX + 8192, 8192, F32, (4, 512))]
    Stmp = [view(RX + 16384, 2048), view(RX + 18432, 2048)]
    h2T = view(RX + 20480, 8192, BF16, (8, 512))
    end_mix = RZ + 16384
    assert end_mix <= ARENA_BYTES, end_mix
    xb1_off = base_mix - 16384
    xb0_off = base_mix - 32768
    stg = [view(xb1_off, 16384, F32, (8, 512)), view(f_off, 16384, F32, (8, 512)),
           view(RX, 16384, F32, (8, 512)), view(RY, 16384, F32, (8, 512))]
    NSTG = 4
    acc = [view(f_off + 16384, 2048), view(f_off + 18432, 2048), view(RZ, 2048), view(RZ + 2048, 2048)]
    mrow = [view(f_off + 20480, 2048, rows=1), view(xb0_off + 4096, 2048, rows=1)]
    badar = [view(xb0_off + 6144, 2048, rows=1), view(xb0_off + 8192, 2048, rows=1)]
    gsl = [view(xb0_off + 10240, 2048, rows=1), view(xb0_off + 12288, 2048, rows=1)]
    csb = view(xb0_off + 14336, 32)
    scf = view(xb0_off + 14368, 32)
    onescol = view(xb0_off + 14400, 32)
    wsp_f = view(RZ + 4096, 4096, F32, (8, 128))
    wpool_f = view(RZ + 8192, 8192)

    def isap(a):
        return not isinstance(a, (int, float))

    def DMA(out, in_, key, eng="sp", **kw):
        return P.op(eng, lambda e: e.dma_start(out=out, in_=in_, **kw), reads=[in_], writes=[out], dkey=key)

    def ACT(out, in_, func, scale=1.0, bias=0.0, accum=None):
        rd = [in_] + [a for a in (scale, bias) if isap(a)]
        wr = [out] + ([accum] if accum is not None else [])
        if accum is not None:
            fn = lambda e: e.activation(out=out, in_=in_, func=func, bias=bias, scale=scale, accum_out=accum)
        else:
            fn = lambda e: e.activation(out=out, in_=in_, func=func, bias=bias, scale=scale)
        return P.op("act", fn, rd, wr)

    def TS(eng, out, in0, s1, s2, op0, op1=None):
        rd = [in0] + [a for a in (s1, s2) if a is not None and isap(a)]
        if op1 is None:
            fn = lambda e: e.tensor_scalar(out=out, in0=in0, scalar1=s1, scalar2=None, op0=op0)
        else:
            fn = lambda e: e.tensor_scalar(out=out, in0=in0, scalar1=s1, scalar2=s2, op0=op0, op1=op1)
        return P.op(eng, fn, rd, [out])

    def STT(eng, out, in0, s, in1, op0, op1):
        rd = [in0, in1] + ([s] if isap(s) else [])
        return P.op(eng, lambda e: e.scalar_tensor_tensor(out=out, in0=in0, scalar=s, in1=in1, op0=op0, op1=op1), rd, [out])

    def TT(eng, out, in0, in1, op):
        return P.op(eng, lambda e: e.tensor_tensor(out=out, in0=in0, in1=in1, op=op), [in0, in1], [out])

    def CP(eng, out, in_):
        return P.op(eng, lambda e: e.tensor_copy(out=out, in_=in_), [in_], [out])

    def MS(eng, out, val):
        return P.op(eng, lambda e: e.memset(out, val), [], [out])

    def MM(out, lhsT, rhs, start, stop):
        return P.op("pe", lambda e: e.matmul(out, lhsT=lhsT, rhs=rhs, start=start, stop=stop), [lhsT, rhs], [out])

    def TR(out, in_):
        return P.op("pe", lambda e: e.transpose(out=out, in_=in_, identity=ident), [in_, ident], [out])

    bank = [0]

    def nbank(align=1):
        b = bank[0]
        if align > 1 and b % align:
            b += align - b % align
        b %= 8
        bank[0] = b + 1
        return b

    P.barrier_keys.add("const")

    def unit_src(u):
        if u < U_DN:
            return wun_d[u], 8
        h, kg = divmod(u - U_DN, 3)
        nk = 8 if kg < 2 else 6
        return wdn_d[h, :, kg * 8:kg * 8 + nk, :], nk

    cvt_order = [4, 5, 2, 3, 0, 1, 6, 7, 8, 9, 10, 11, 12, 13, 14, 15] + list(range(U_UP, NUNIT))

    def issue_cast(u):
        src, nk = unit_src(u)
        DMA(wscr[u].rearrange("p (a b) -> p a b", a=8)[:, 0:nk, :], src, "cv%d" % u, eng="pool")

    DMA(csb, c_d, "cvec")
    MS("pool", mh[:, 0:4], -0.5)
    MS("pool", ones_f, 1.0)
    MS("pool", onescol[:, 0:1], 1.0)
    MS("pool", ones_bf, 1.0)
    MS("pool", Hh, 0.0)
    ACT(scf[:, 0:8], csb[:, 0:8], AF.Silu)
    GS1, SHC1, GS2, SHC2 = [modcols[:, i * 8:(i + 1) * 8] for i in range(4)]

    def load_constants():
        DMA(ident, ident_d, "const")
        DMA(cmask[:, 0:1], cmask_d, "const")
        DMA(invc, invc_d.rearrange("p (a b) -> p a b", a=4), "const")
        DMA(lng_b, lng_d.partition_broadcast(128), "const")
        DMA(lnb_b, lnb_d.partition_broadcast(128), "const")
        DMA(gfin_b, gfin_d.partition_broadcast(128), "const")
        DMA(bpool[:, 0:8], bpool_d, "const")
        DMA(pscale[:, 0:8], pscale_d, "const")
        DMA(hbgate[:, 0:16], bgate_d, "const")
        DMA(convw, convw_d, "const")
        DMA(convb[:, 0:NCH_UP], convb_d, "const")
        MS("pool", bs_f, 0.0)
        DMA(bs_f[0:1, :], bsp_d, "bsf0")
        DMA(bs_f[32:33, :], bsp_d, "bsf1")
        DMA(wsp_f, wsp_d, "const")
        DMA(wpool_f[:, 0:2048], wpool_d.rearrange("p a b c -> p (a b c)"), "const")
        CP("dve", wmT, wsp_f)
        MS("pool", wmT[64:128, :, 0:64], 0.0)
        MS("pool", bs2, 0.0)
        CP("dve", bs2[0:1, :], bs_f[0:1, :])
        CP("dve", bs_t[32:33, :], bs_f[32:33, :])
        TT("dve", bs_f[32:33, :], bs_f[32:33, :], bs_t[32:33, :], ALU.subtract)
        CP("dve", bs2[32:33, :], bs_f[32:33, :])
        TS("dve", hbgate[:, 0:16], hbgate[:, 0:16], 0.5, None, ALU.mult)
        CP("dve", wpool.rearrange("p a b c -> p (a b c)"), wpool_f[:, 0:2048])

    wada_key = ["stg%d" % i for i in range(NSTG)]
    mod_bank = {}

    def mod_chain(g):
        s_ = g % NSTG
        kind, half = divmod(g, 2)
        DMA(stg[s_].rearrange("p a b -> p (a b)"), wada_d[g].rearrange("p a b -> p (a b)"), wada_key[s_])
        DMA(badar[g % 2], bada_d[:, g * 512:(g + 1) * 512], "bada%d" % (g % 2))
        if kind in (1, 4):
            DMA(gsl[g % 2], (g1_d if kind == 1 else g2_d)[:, half * 512:(half + 1) * 512], "gsl%d" % (g % 2))
        a = acc[s_]
        NDVE = 6
        for kc in range(NDVE):
            for hh in range(2):
                cs = slice(hh * 256, (hh + 1) * 256)
                if kc == 0:
                    TS("dve", a[:, cs], stg[s_][:, 0, cs], scf[:, 0:1], None, ALU.mult)
                else:
                    STT("dve", a[:, cs], stg[s_][:, kc, cs], scf[:, kc:kc + 1], a[:, cs], ALU.mult, ALU.add)
        b = nbank()
        for kc in range(NDVE, 8):
            MM(ps[0:1, b, :], scf[:, kc:kc + 1], stg[s_][:, kc, :], kc == NDVE, False)
        MM(ps[0:1, b, :], onescol[:, 0:1], a[:, 0:512], False, True)
        mod_bank[g] = b

    def mod_fin(g):
        kind, half = divmod(g, 2)
        b = mod_bank[g]
        row = mrow[g % 2]
        TT("dve", row[:, 0:512], ps[0:1, b, :], badar[g % 2][:, 0:512], ALU.add)
        if kind in (1, 4):
            STT("dve", row[:, 0:512], row[:, 0:512], 1.0, gsl[g % 2][:, 0:512], ALU.add, ALU.mult)
        if kind == 2:
            TS("dve", row[:, 0:512], row[:, 0:512], 0.5, None, ALU.mult)
        if kind in (2, 5):
            b2 = nbank()
            MM(ps[:, b2, :], ones_f[:, 0:128], row[:, 0:512], True, True)
            dst = gt1h_b if kind == 2 else gt2_b
            CP("dve", dst[:, half * 512:(half + 1) * 512], ps[:, b2, :])
        else:
            col0 = {1: 0, 0: 8, 4: 16, 3: 24}[kind] + half * 4
            b2 = nbank()
            for c in range(4):
                MM(ps[:, b2, c:c + 1], row[:, c * 128:(c + 1) * 128], ones_f[:, 0:1], True, True)
            CP("dve", modcols[:, col0:col0 + 4], ps[:, b2, 0:4])

    for g in range(12):
        mod_chain(g)
        if g >= 1:
            mod_fin(g - 1)
    mod_fin(11)
    load_constants()


    MIX_UNITS = [2, 3, 4, 5, 0, 1, 8, 9, 10, 11, 6, 7, 12, 13, 14, 15]
    UP_UNITS = list(range(U_UP, U_DN))
    DN_UNITS = list(range(U_DN, NUNIT))
    seq = []
    for t in range(1, n_tiles):
        seq += [(t, u) for u in MIX_UNITS[:14]]
        if t >= 2:
            seq += [(t - 1, u) for u in DN_UNITS[:3]]
        seq += [(t, u) for u in MIX_UNITS[14:]]
        if t >= 2:
            seq += [(t - 1, u) for u in DN_UNITS[3:]]
        if t >= 1:
            seq += [(t, u) for u in UP_UNITS]
    if n_tiles >= 2:
        seq += [(n_tiles - 1, u) for u in DN_UNITS]
    loaded = [0]
    usepos = [0]

    def ensure_loaded(upto):
        while loaded[0] < min(upto, len(seq)):
            i = loaded[0]
            (tt_, u) = seq[i]
            src, nk = unit_src(u)
            if tt_ <= 1:
                DMA(ring[i % NSLOT].rearrange("p a b -> p (a b)")[:, 0:nk * 512], src.rearrange("p a b -> p (a b)"),
                    "ringc%d" % (i % NSLOT), eng="pool")
            else:
                DMA(ring[i % NSLOT].rearrange("p a b -> p (a b)")[:, 0:nk * 512], wscr[u][:, 0:nk * 512], "ring%d" % (i % NSLOT))
            loaded[0] += 1

    released = [0]

    def next_unit(expect_u):
        i = usepos[0]
        assert seq[i][1] == expect_u, (i, seq[i], expect_u)
        assert i < released[0] + NSLOT
        ensure_loaded(i + 1)
        usepos[0] += 1
        return ring[i % NSLOT]

    def rel(n=1):
        for i in range(released[0], released[0] + n):
            (tt_, u) = seq[i]
            if tt_ == 1 and n_tiles > 2:
                nk = 6 if (u >= U_DN and (u - U_DN) % 3 == 2) else 8
                DMA(wscr[u][:, 0:nk * 512], ring[i % NSLOT].rearrange("p a b -> p (a b)")[:, 0:nk * 512], "wst%d" % (i % NSLOT))
        released[0] += n
        ensure_loaded(released[0] + NSLOT)

    def nb_of(t):
        return 1 if t == 0 else 4

    def stats_xn(xt, nb, k):
        for b in range(nb):
            ACT(xn[:, b, :], xt[:, b, :], AF.Square, accum=ssq[k][:, b:b + 1])
        TS("dve", msq[k][:, 0:nb], ssq[k][:, 0:nb], 1.0 / D, EPS, ALU.mult, ALU.add)
        TT("pool", rstd[k][:, 0:nb], msq[k][:, 0:nb], mh[:, 0:nb], ALU.pow)
        for b in range(nb):
            TS("dve", xn[:, b, :], xt[:, b, :], rstd[k][:, b:b + 1], None, ALU.mult)

    def transposes_to(hdst, nb, gs, shc, last2=False):
        ntok = nb * 128
        for c in range(8):
            bk = nbank()
            for b in range(nb):
                TR(ps[:, bk, b * 128:(b + 1) * 128], xn[:, b, c * 128:(c + 1) * 128])
            if last2:
                ACT(hdst[:, c, :], ps[:, bk, ntok - 2:ntok], AF.Identity, scale=gs[:, c:c + 1], bias=shc[:, c:c + 1])
            else:
                ACT(hdst[:, c, 0:ntok], ps[:, bk, 0:ntok], AF.Identity, scale=gs[:, c:c + 1], bias=shc[:, c:c + 1])

    def fm_group(wslot, j, src, ntok):
        bk = nbank()
        for kc in range(8):
            MM(ps[:, bk, 0:ntok], wslot[:, kc, j * 128:(j + 1) * 128], src[:, kc, 0:ntok], kc == 0, kc == 7)
        return bk

    class _NS:
        pass

    BIG = _NS()
    BIG.zb, BIG.ptmp_d, BIG.ptmp_p, BIG.pooled, BIG.vge, BIG.vg, BIG.u_b = zb, ptmp_d, ptmp_p, pooled, vge, vg, u_b
    BIG.yb, BIG.th, BIG.ya, BIG.t2, BIG.t1, BIG.merged, BIG.hT, BIG.tH = yb, th, ya, t2, t1, merged, hT, t1
    BIG.bnst, BIG.bnmv, BIG.lnms, BIG.lnrs, BIG.lnnm = bnst, bnmv, lnms, lnrs, lnnm
    BIG.get, BIG.rel = next_unit, rel
    B0 = _NS()
    o = f_off
    B0.zb = view(o, 4608, F32, (8, 144)); o += 4608
    B0.ptmp_d = [view(o, 576), view(o + 576, 576)]; o += 1152
    B0.ptmp_p = B0.ptmp_d
    B0.pooled = view(o, 2048, BF16, (8, 128)); o += 2048
    B0.ya = view(o, 2048, BF16, (8, 128)); o += 2048
    B0.t2 = view(o, 4096, F32, (8, 128))
    B0.tH = [view(o, 2048), view(o + 2048, 2048)]; o += 4096
    B0.t1 = [view(o, 512), view(o + 512, 512)]; o += 1024
    B0.vge = [view(o, 4096)]; o += 4096
    B0.vg = view(o, 2048, BF16, (1, D)); o += 2048
    assert o <= f_off + 22528
    o = xb0_off + 4096
    B0.u_b = view(o, 2048, BF16, (8, 128)); o += 2048
    B0.merged = view(o, 2048, BF16, (8, 128)); o += 2048
    B0.yb = view(o, 2048, BF16, (8, 128)); o += 2048
    B0.th = view(o, 2048, BF16, (8, 128)); o += 2048
    B0.hT = view(o, 2048, BF16, (8, 128)); o += 2048
    B0.bnst = view(o, 64, F32, (1, 2, 8))[:, :, :, 0:6]; o += 64
    B0.bnmv = view(o, 32, F32, (1, 8))[:, :, 0:2]; o += 32
    B0.lnms, B0.lnrs, B0.lnnm = view(o, 32), view(o + 32, 32), view(o + 64, 32); o += 96
    assert o <= xb0_off + 16384
    _shared = {}

    def _get0(u):
        _shared[u] = next_unit(u)
        return _shared[u]

    B0.get, B0.rel = _get0, (lambda n=1: None)

    taps = {}

    def tap(name, ap, shape):
        if name in debug_taps:
            d = nc.dram_tensor("dbg_" + name, list(shape), F32 if ap.dtype == F32 else BF16, kind="ExternalOutput").ap()
            taps[name] = P.op("sp", lambda e: e.dma_start(out=d, in_=ap), reads=[ap], writes=[], dkey="tap_" + name)

    out_ops = []

    def load_x(t):
        xt = xb[t % 2]
        if t == 0:
            DMA(xt[:, 0:1, :], x_d[0:1].rearrange("b p d -> p b d"), "x%d" % (t % 2))
        else:
            b0 = 1 + 4 * (t - 1)
            DMA(xt, x_d[b0:b0 + 4].rearrange("b p d -> p b d"), "x%d" % (t % 2))

    def norm1(t):
        nb = nb_of(t)
        stats_xn(xb[t % 2], nb, 0)
        transposes_to(hT, nb, GS1, SHC1)

    def gates(B, first_unit, ntok):
        for q in range(2):
            w = B.get(first_unit + q)
            for j in range(4):
                m = q * 4 + j
                col = (first_unit - U_GATE) * 4 + m
                bk = fm_group(w, j, B.hT, ntok)
                ACT(B.th[:, m, 0:ntok], ps[:, bk, 0:ntok], AF.Tanh, scale=0.5, bias=hbgate[:, col:col + 1])
            B.rel()

    def mixer(t, B):
        zb, ptmp_d, ptmp_p, pooled, vge, vg, u_b, yb, th, ya, t2, t1, merged, hT_ = (
            B.zb, B.ptmp_d, B.ptmp_p, B.pooled, B.vge, B.vg, B.u_b, B.yb, B.th, B.ya, B.t2, B.t1, B.merged, B.hT)
        bnst_, bnmv_, lnms_, lnrs_, lnnm_, tH = B.bnst, B.bnmv, B.lnms, B.lnrs, B.lnnm, B.tH
        halo = (t == 0)
        nb = nb_of(t)
        ntok = nb * 128
        xt = xb[t % 2]
        dbg = (t == 1)
        if dbg:
            tap("hT1", hT_, (128, 8, 512))
        wv = [B.get(2), B.get(3)]
        for b in range(nb):
            vb_ = vge[b % len(vge)]
            for h in range(2):
                bk = nbank()
                for kc in range(8):
                    MM(ps[:, bk, :], hT_[:, kc, b * 128:(b + 1) * 128], wv[h][:, kc, :], kc == 0, kc == 7)
                ACT(vb_[:, h * 512:(h + 1) * 512], ps[:, bk, :], AF.Gelu)
            for h in range(2):
                P.op("dve", lambda e, b=b, h=h, vb_=vb_: e.bn_stats(out=bnst_[:, b, h, :], in_=vb_[:, h * 512:(h + 1) * 512]),
                     [vb_[:, h * 512:(h + 1) * 512]], [bnst_[:, b, h, :]])
            P.op("dve", lambda e, b=b: e.bn_aggr(out=bnmv_[:, b, :], in_=bnst_[:, b, :, :]), [bnst_[:, b, :, :]], [bnmv_[:, b, :]])
            TS("dve", lnms_[:, b:b + 1], bnmv_[:, b, 1:2], EPS, None, ALU.add)
            TT("pool", lnrs_[:, b:b + 1], lnms_[:, b:b + 1], mh[:, 0:1], ALU.pow)
            STT("dve", lnnm_[:, b:b + 1], bnmv_[:, b, 0:1], -1.0, lnrs_[:, b:b + 1], ALU.mult, ALU.mult)
            ACT(vb_[:, 0:D], vb_[:, 0:D], AF.Identity, scale=lnrs_[:, b:b + 1], bias=lnnm_[:, b:b + 1])
            TT("dve", vb_[:, 0:D], vb_[:, 0:D], lng_b[:, 0:D], ALU.mult)
            TT("dve", vg[:, b, :], vb_[:, 0:D], lnb_b[:, 0:D], ALU.add)
        B.rel(2)
        yield
        for half in range(2):
            w = B.get(4 + half)
            for j in range(4):
                m = half * 4 + j
                bk = fm_group(w, j, hT_, ntok)
                ACT(zb[:, m, 16:16 + ntok], ps[:, bk, 0:ntok], AF.Copy)
            B.rel()
        CP("dve", zb[:, :, 0:16], zpre)
        for m in range(8):
            gi = m // 2
            Z = zb[:, m, :]
            L = 16 + ntok
            srcv = Z
            lag = 1
            k = 0
            vs = 0
            ceng = "pool" if (t >= 2 and m % 2 == 1) else "dve"
            ptmp = ptmp_p if ceng == "pool" else ptmp_d
            while lag < WINS[gi]:
                dst = ptmp[k % 2]
                vs2 = vs + lag
                TT(ceng, dst[:, vs2:L], srcv[:, vs2:L], srcv[:, vs2 - lag:L - lag], ALU.add)
                srcv = dst
                vs = vs2
                lag *= 2
                k += 1
            STT("dve", pooled[:, m, 0:ntok], srcv[:, 16:L], 1.0 / WINS[gi], Z[:, 16:L], ALU.mult, ALU.subtract)
            if t == 1:
                TT("dve", ptmp[(k + 1) % 2][:, 0:16], srcv[:, 16:32], invc[:, gi, :], ALU.mult)
                TT("dve", pooled[:, m, 0:16], ptmp[(k + 1) % 2][:, 0:16], Z[:, 16:32], ALU.subtract)
        if dbg:
            tap("pooled", pooled, (128, 8, 512))
        if halo:
            TS("dve", zpre, zb[:, :, ntok:ntok + 16], cmask[:, 0:1], None, ALU.mult)
        else:
            CP("dve", zpre, zb[:, :, ntok:ntok + 16])
        yield
        for half in range(2):
            w = B.get(0 + half)
            for j in range(4):
                m = half * 4 + j
                bk = fm_group(w, j, hT_, ntok)
                ACT(u_b[:, m, 0:ntok], ps[:, bk, 0:ntok], AF.Gelu)
            B.rel()
        yield
        gates(B, U_GATE + 2, ntok)
        yield
        for m in range(8):
            g, dc = divmod(m, 2)
            bk = nbank()
            for cc in range(2):
                MM(ps[:, bk, 0:ntok], wpool[:, g, cc, dc * 128:(dc + 1) * 128], pooled[:, g * 2 + cc, 0:ntok], cc == 0, cc == 1)
            TS("dve", yb[:, m, 0:ntok], ps[:, bk, 0:ntok], bpool[:, m:m + 1], pscale[:, m:m + 1], ALU.add, ALU.mult)
        for g in range(8):
            bk = nbank()
            for b in range(nb):
                o = ps[:, bk, b * 128:(b + 1) * 128]
                MM(o, vg[:, b, g * 128:(g + 1) * 128], wmT[:, g, :], True, False)
                MM(o, ones_bf[:, 0:128], bs2[:, g * 128:(g + 1) * 128], False, True)
            TT("dve", ya[:, g, 0:ntok], u_b[:, g, 0:ntok], ps[:, bk, 0:ntok], ALU.mult)
        yield
        for half in range(2):
            w = B.get(U_PB + half)
            for j in range(4):
                m = half * 4 + j
                bk = fm_group(w, j, yb, ntok)
                STT("dve", t2[:, m, 0:ntok], th[:, m, 0:ntok], 1.0, ps[:, bk, 0:ntok], ALU.add, ALU.mult)
            B.rel()
        yield
        gates(B, U_GATE, ntok)
        yield
        for half in range(2):
            w = B.get(U_PA + half)
            for j in range(4):
                m = half * 4 + j
                bk = fm_group(w, j, ya, ntok)
                tt = t1[m % 2]
                STT("dve", tt[:, 0:ntok], th[:, m, 0:ntok], 1.0, ps[:, bk, 0:ntok], ALU.add, ALU.mult)
                TT("dve", merged[:, m, 0:ntok], tt[:, 0:ntok], t2[:, m, 0:ntok], ALU.add)
            B.rel()
        if dbg:
            tap("merged", merged, (128, 8, 512))
        yield
        if t >= 2:
            ffn_down_half(t - 1, 0)
        wo = [B.get(U_OUT), B.get(U_OUT + 1)]
        for b in range(nb):
            for h in range(2):
                bk = nbank()
                for kc in range(8):
                    MM(ps[:, bk, :], merged[:, kc, b * 128:(b + 1) * 128], wo[h][:, kc, :], kc == 0, kc == 7)
                xs = xt[:, b, h * 512:(h + 1) * 512]
                tt = tH[(b * 2 + h) % 2]
                TT("dve", tt[:, 0:512], ps[:, bk, :], gt1h_b[:, h * 512:(h + 1) * 512], ALU.mult)
                TT("dve", xs, xs, tt[:, 0:512], ALU.add)
        B.rel(2)
        if dbg:
            tap("xmid", xt, (128, 4, 1024))

    def ffn_down_half(t, h, fin_blocks=False):
        nb = nb_of(t)
        xt = xb[t % 2]
        if True:
            ws = [next_unit(U_DN + h * 3 + kg) for kg in range(3)]
            for b in range(nb):
                bk = nbank()
                for kc in range(22):
                    MM(ps[:, bk, :], f_b[:, kc, b * 128:(b + 1) * 128], ws[kc // 8][:, kc % 8, :], kc == 0, kc == 21)
                xs = xt[:, b, h * 512:(h + 1) * 512]
                tt = dtmp[(h * nb + b) % 2]
                TT("dve", tt[:, 0:512], ps[:, bk, :], gt2_b[:, h * 512:(h + 1) * 512], ALU.mult)
                TT("pool", xs, xs, tt[:, 0:512], ALU.add)
                if fin_blocks:
                    ACT(otmp[:, 0:D], xt[:, b, :], AF.Square, accum=ssq[2][:, b:b + 1])
                    TS("dve", msq[2][:, b:b + 1], ssq[2][:, b:b + 1], 1.0 / D, EPS, ALU.mult, ALU.add)
                    TT("pool", rstd[2][:, b:b + 1], msq[2][:, b:b + 1], mh[:, 0:1], ALU.pow)
                    STT("dve", xt[:, b, :], xt[:, b, :], rstd[2][:, b:b + 1], gfin_b[:, 0:D], ALU.mult, ALU.mult)
                    bo = 4 * (t - 1) + b
                    out_ops.append(DMA(out_d[bo:bo + 1].rearrange("b p d -> p b d"), xt[:, b:b + 1, :], "stl%d" % b))
            rel(3)

    def final_and_store(t):
        nb = nb_of(t)
        xt = xb[t % 2]
        for b in range(nb):
            ACT(otmp[:, 0:D], xt[:, b, :], AF.Square, accum=ssq[2][:, b:b + 1])
        TS("dve", msq[2][:, 0:nb], ssq[2][:, 0:nb], 1.0 / D, EPS, ALU.mult, ALU.add)
        TT("pool", rstd[2][:, 0:nb], msq[2][:, 0:nb], mh[:, 0:nb], ALU.pow)
        for b in range(nb):
            STT("dve", xt[:, b, :], xt[:, b, :], rstd[2][:, b:b + 1], gfin_b[:, 0:D], ALU.mult, ALU.mult)
        b0 = 4 * (t - 1)
        out_ops.append(DMA(out_d[b0:b0 + 4].rearrange("b p d -> p b d"), xt, "st%d" % (t % 2)))

    def ffn_up(t, hook_a=None, hook_b=None):
        halo = (t == 0)
        nb = nb_of(t)
        ntok = nb * 128
        assert not halo
        def halo_contrib(src, r):
            TT("dve", htmp[:, r], src[:, r, 0], convw[:, r, 0], ALU.mult)
            TT("dve", Hh[:, r, 0], src[:, r, 1], convw[:, r, 1], ALU.mult)
            TT("dve", Hh[:, r, 0], Hh[:, r, 0], htmp[:, r], ALU.add)
            TT("dve", Hh[:, r, 1], src[:, r, 1], convw[:, r, 0], ALU.mult)
        if t >= 2:
            halo_contrib(ph, slice(0, NCH_UP))
        for uu in range(11):
            if uu == 2 and hook_a is not None:
                hook_a()
            if uu == 7 and hook_b is not None:
                hook_b()
            w = next_unit(U_UP + uu)
            A = Abuf[uu % 2]
            pend = None
            if t == 1:
                bh = nbank()
                r = slice(uu * 4, uu * 4 + 4)
                for j in range(4):
                    for kc in range(8):
                        MM(ps[:, bh, 2 * j:2 * j + 2], w[:, kc, j * 128:(j + 1) * 128], hh2[:, kc, :], kc == 0, kc == 7)
                TS("dve", ph0[:, r, :], ps[:, bh, 0:8].rearrange("p (a b) -> p a b", a=4), cmask[:, 0:1], None, ALU.mult)
                halo_contrib(ph0, r)
            for j in range(4):
                q = uu * 4 + j
                bk = nbank()
                for kc in range(8):
                    MM(ps[:, bk, 0:ntok], w[:, kc, j * 128:(j + 1) * 128], h2T[:, kc, 0:ntok], kc == 0, kc == 7)
                if not halo:
                    ACT(A[:, j, 0:ntok], ps[:, bk, 0:ntok], AF.Identity, scale=convw[:, q, 2:3], bias=convb[:, q:q + 1])
                ACT(ph[:, q, :], ps[:, bk, ntok - 2:ntok], AF.Copy)
                if not halo:
                    STT("dve", A[:, j, 1:ntok], ps[:, bk, 0:ntok - 1], convw[:, q, 1:2], A[:, j, 1:ntok], ALU.mult, ALU.add)
                    if pend is not None:
                        pend()
                    pend = (lambda j=j, q=q, bk=bk: STT("dve", A[:, j, 2:ntok], ps[:, bk, 0:ntok - 2], convw[:, q, 0:1],
                                                        A[:, j, 2:ntok], ALU.mult, ALU.add))
            if pend is not None:
                pend()
                pend = None
            rel()
            if not halo:
                TT("dve", A[:, :, 0:2], A[:, :, 0:2], Hh[:, uu * 4:(uu + 1) * 4, :], ALU.add)
                for jj in range(2):
                    S = Stmp[jj]
                    ACT(S[:, 0:ntok], A[:, 2 * jj, 0:ntok], AF.Silu)
                    TT("pool", f_b[:, uu * 2 + jj, 0:ntok], S[:, 0:ntok], A[:, 2 * jj + 1, 0:ntok], ALU.mult)

    dtmp = [view(RX + 16384, 2048), view(RX + 18432, 2048)]
    otmp = view(RX, 4096)
    assert n_tiles >= 2
    MS("pool", zpre, 0.0)
    load_x(0)
    load_x(1)
    ensure_loaded(NSLOT)
    stats_xn(xb[0], 1, 0)
    transposes_to(B0.hT, 1, GS1, SHC1)
    stats_xn(xb[1], 4, 0)
    transposes_to(hT, 4, GS1, SHC1)
    BJ = _NS()
    BJ.__dict__.update(BIG.__dict__)
    BJ.get = lambda u: _shared.pop(u)
    g0, g1 = mixer(0, B0), mixer(1, BJ)
    while True:
        d0 = next(g0, "done")
        d1 = next(g1, "done")
        assert (d0 == "done") == (d1 == "done")
        if d0 == "done":
            break
    assert not _shared
    stats_xn(xb[0], 1, 1)
    transposes_to(hh2, 1, GS2, SHC2, last2=True)
    for t in range(1, n_tiles):
        nb = 4
        if t >= 2:
            for _ in mixer(t, BIG):
                pass
        stats_xn(xb[t % 2], nb, 1)
        if t >= 2:
            ffn_down_half(t - 1, 1)
            final_and_store(t - 1)
        if t + 1 < n_tiles:
            load_x(t + 1)
        transposes_to(h2T, nb, GS2, SHC2)
        if t + 1 < n_tiles:
            ha = lambda t=t: stats_xn(xb[(t + 1) % 2], 4, 0)
            hb = lambda t=t: transposes_to(hT, 4, GS1, SHC1)
        else:
            ha = hb = None
        ffn_up(t, hook_a=ha, hook_b=hb)
    if n_tiles >= 2:
        ffn_down_half(n_tiles - 1, 0)
        ffn_down_half(n_tiles - 1, 1, fin_blocks=True)
    assert usepos[0] == len(seq) and released[0] == len(seq), (usepos[0], released[0], len(seq))
    P.finish(out_ops + list(taps.values()))
    P.build()
    return nc, P


def _fm(v, nch):
    return np.ascontiguousarray(np.asarray(v, np.float32).reshape(nch, 128).T)


def _kunits(w):
    K, N = w.shape
    return np.ascontiguousarray(w.reshape(K // 128, 128, N // 512, 512).transpose(2, 1, 0, 3))


def prepare_inputs(x, c, w_ada, b_ada, g_norm1, w_in, ln_v_g, ln_v_b, w_spatial, b_spatial,
                   w_pool, b_pool, pool_scale, w_proj_a, w_proj_b, w_gate, b_gate, w_out,
                   g_norm2, w_up, conv_w, conv_b, w_down, g_final):
    f = lambda a: np.asarray(a, np.float32)
    x, c = f(x), f(c)
    w_up0 = f(w_up)[0]
    perm = np.empty(NCH_UP, np.int64)
    perm[0::2] = np.arange(22)
    perm[1::2] = 22 + np.arange(22)
    colperm = (perm[:, None] * 128 + np.arange(128)[None, :]).reshape(-1)
    w_up_p = w_up0[:, colperm]
    units = np.concatenate([
        _kunits(f(w_in)[0]), _kunits(f(w_gate)[0]), _kunits(f(w_proj_b)[0]), _kunits(f(w_proj_a)[0]),
        _kunits(f(w_out)[0]), _kunits(w_up_p)], axis=0)
    assert units.shape[0] == U_DN
    wdn = np.ascontiguousarray(f(w_down)[0].reshape(22, 128, 2, 512).transpose(2, 1, 0, 3))
    shared = {
        "ident": np.eye(128, dtype=np.float32),
        "w_ada": np.ascontiguousarray(f(w_ada)[0].reshape(8, 128, 12, 512).transpose(2, 1, 0, 3)),
        "b_ada": f(b_ada).reshape(1, 6 * D),
        "g_norm1": f(g_norm1).reshape(1, D),
        "g_norm2": f(g_norm2).reshape(1, D),
        "w_units": units,
        "w_down": wdn,
        "ln_v_g": f(ln_v_g).reshape(D),
        "ln_v_b": f(ln_v_b).reshape(D),
        "g_final": f(g_final).reshape(D),
        "w_spT": np.ascontiguousarray(f(w_spatial)[0].transpose(2, 0, 1)),
        "b_sp": f(b_spatial).reshape(1, D),
        "w_pool": np.ascontiguousarray(f(w_pool)[0].reshape(4, 2, 128, 256).transpose(2, 0, 1, 3)),
        "b_pool": _fm(f(b_pool).reshape(-1), 8),
        "pool_scale": _fm(f(pool_scale).reshape(-1), 8),
        "b_gate": _fm(f(b_gate).reshape(-1), 16),
        "conv_w": np.ascontiguousarray(f(conv_w)[0][:, colperm].reshape(3, NCH_UP, 128).transpose(2, 1, 0)),
        "conv_b": _fm(f(conv_b)[0][colperm], NCH_UP),
    }
    in_maps = []
    for core in range(NCORE):
        b, half = divmod(core, 2)
        s0 = half * TOK
        xc = np.zeros((NBLK + 1, 128, D), np.float32)
        if half:
            xc[0] = x[b, s0 - 128:s0]
        xc[1:] = x[b, s0:s0 + TOK].reshape(NBLK, 128, D)
        invc = np.empty((4, 16), np.float32)
        for gi, w in enumerate(WINS):
            for j in range(16):
                invc[gi, j] = 1.0 / (min(j + 1, w) if half == 0 else w)
        m = dict(shared)
        m["x"] = xc
        m["c"] = _fm(c[b], 8)
        m["cmask"] = np.full((128, 1), float(half), np.float32)
        m["invc"] = np.ascontiguousarray(np.broadcast_to(invc.reshape(1, 64), (128, 64)))
        in_maps.append(m)
    return in_maps


_CACHE = {}


def kernel(**inputs):
    in_maps = prepare_inputs(**inputs)
    if "nc" not in _CACHE:
        _CACHE["nc"] = build_program()[0]
    res = run_bass_kernel_spmd(_CACHE["nc"], in_maps, core_ids=list(range(NCORE)))
    out = np.empty((BATCH, SEQ, D), np.float32)
    for core in range(NCORE):
        b, half = divmod(core, 2)
        out[b, half * TOK:(half + 1) * TOK] = np.asarray(res.results[core]["out"]).reshape(TOK, D)
    return out
```

```python
import contextlib
import numpy as np
import concourse.bass as bass
import concourse.mybir as mybir
from concourse.bass_utils import run_bass_kernel_spmd

F32 = mybir.dt.float32
BF16 = mybir.dt.bfloat16
AF = mybir.ActivationFunctionType
ALU = mybir.AluOpType

D = 1024
SEQ = 4096
BATCH = 4
NCORE = 8
TOK = 2048
NBLK = TOK // 128
DFF = 2816
NCH_UP = 44
EPS = 1e-6
WINS = (2, 4, 8, 16)
NSLOT = 5
ENGS = ("pe", "act", "dve", "pool", "sp")

SPACE = {"arena": 0, "ps": 1 << 24, "wscr": 1 << 28}


def _esize(dt):
    return 2 if dt == BF16 else 4


def ap_ivs(ap):
    name = ap.tensor.name
    if name not in SPACE:
        return []
    base = SPACE[name]
    es = _esize(ap.dtype)
    dims = list(ap.ap)
    if name == "wscr":
        off = ap.offset
    else:
        pstride = dims[0][0]
        off = ap.offset % pstride if pstride > 0 else ap.offset
        dims = dims[1:]

    def rec(dims):
        if not dims:
            return [(0, 1)]
        (s, n) = dims[0]
        inner = rec(dims[1:])
        ext = max(h for _, h in inner)
        lo0 = min(l for l, _ in inner)
        if n == 1:
            return inner
        if s <= 0:
            return [(lo0, ext)]
        if s < ext or n > 64 or len(inner) > 1:
            return [(lo0, ext + (n - 1) * s)]
        if s == ext and lo0 == 0:
            return [(0, ext + (n - 1) * s)]
        return [(i * s + lo0, i * s + ext) for i in range(n)]

    ivs = [(base + (off + l) * es, base + (off + h) * es) for (l, h) in rec(dims)]
    if name == "ps":
        ivs = sorted(set((base + ((lo - base) // 2048) * 2048, base + ((hi - 1 - base) // 2048 + 1) * 2048) for lo, hi in ivs))
    return ivs


class Op:
    __slots__ = ("eng", "fn", "deps", "signal", "sem", "val", "dkey", "idx", "clock", "waits")

    def __init__(self, eng, fn, dkey):
        self.eng = eng
        self.fn = fn
        self.deps = []
        self.signal = False
        self.sem = None
        self.val = 0
        self.dkey = dkey
        self.clock = None
        self.waits = []


class Prog:
    def __init__(self, nc):
        self.nc = nc
        self.ops = []
        self.W = []
        self.R = []
        self.finals = []
        self.barrier_keys = set()

    def op(self, eng, fn, reads=(), writes=(), dkey=None):
        o = Op(eng, fn, dkey)
        o.idx = len(self.ops)
        riv = []
        for a in reads:
            riv.extend(ap_ivs(a))
        wiv = []
        for a in writes:
            wiv.extend(ap_ivs(a))
        deps = set()
        W, R = self.W, self.R
        PS0 = SPACE["ps"]
        PS1 = PS0 + (1 << 20)
        for (lo, hi) in riv:
            for (l, h, q) in W:
                if l < hi and lo < h:
                    deps.add(q)
            if PS0 <= lo < PS1:
                for (l, h, q) in R:
                    if l < hi and lo < h and q.eng != eng:
                        deps.add(q)
        for (lo, hi) in wiv:
            for (l, h, q) in W:
                if l < hi and lo < h:
                    deps.add(q)
            for (l, h, q) in R:
                if l < hi and lo < h:
                    deps.add(q)
        for (lo, hi) in wiv:
            W = [t for t in W if not (lo <= t[0] and t[1] <= hi)]
            R = [t for t in R if not (lo <= t[0] and t[1] <= hi)]
            W.append((lo, hi, o))
        is_cmp = dkey is None
        for (lo, hi) in riv:
            if is_cmp:
                R = [t for t in R if not (t[2].eng == eng and t[2].dkey is None and lo <= t[0] and t[1] <= hi)]
            R.append((lo, hi, o))
        self.W, self.R = W, R
        for d in deps:
            if d is o:
                continue
            if d.eng == "pe" and eng == "pe" and d.dkey is None and dkey is None:
                continue
            o.deps.append(d)
            d.signal = True
        if dkey is not None:
            o.signal = True
        self.ops.append(o)
        return o

    def finish(self, ops):
        for o in ops:
            o.signal = True
            self.finals.append(o)

    def build(self):
        nc = self.nc
        st = contextlib.ExitStack()
        eng_sem, eng_cnt = {}, {e: 0 for e in ENGS}
        dma_sem, dma_cnt = {}, {}
        for o in self.ops:
            if not o.signal:
                continue
            if o.dkey is not None:
                if o.dkey not in dma_sem:
                    dma_sem[o.dkey] = st.enter_context(nc.semaphore("d_" + str(o.dkey)))
                    dma_cnt[o.dkey] = 0
                dma_cnt[o.dkey] += 16
                o.sem, o.val = dma_sem[o.dkey], dma_cnt[o.dkey]
            else:
                if o.eng not in eng_sem:
                    eng_sem[o.eng] = st.enter_context(nc.semaphore("s_" + o.eng))
                eng_cnt[o.eng] += 1
                o.sem, o.val = eng_sem[o.eng], eng_cnt[o.eng]
        for o in self.ops:
            if o.signal and o.dkey in self.barrier_keys:
                o.val = dma_cnt[o.dkey]
        clock = {e: {} for e in ENGS}
        nwait = 0
        for o in self.ops:
            ck = clock[o.eng]
            for d in sorted(o.deps, key=lambda d: -d.idx):
                key = d.sem.num
                if ck.get(key, 0) >= d.val:
                    continue
                o.waits.append((d.sem, d.val))
                nwait += 1
                ck[key] = d.val
                if d.clock is not None:
                    for k2, v2 in d.clock.items():
                        if ck.get(k2, 0) < v2:
                            ck[k2] = v2
            if o.signal:
                o.clock = dict(ck)
        self.nwait = nwait
        by_eng = {e: [] for e in ENGS}
        for o in self.ops:
            by_eng[o.eng].append(o)
        finals = self.finals

        def emit(engobj, lst, last=False):
            for o in lst:
                for (s, v) in o.waits:
                    engobj.wait_ge(s, v)
                ins = o.fn(engobj)
                if o.signal:
                    ins.then_inc(o.sem, 16 if o.dkey is not None else 1)
            if last:
                for o in finals:
                    engobj.wait_ge(o.sem, o.val)

        with nc.Block() as block:
            @block.tensor
            def _(e):
                emit(e, by_eng["pe"])

            @block.scalar
            def _(e):
                emit(e, by_eng["act"])

            @block.vector
            def _(e):
                emit(e, by_eng["dve"])

            @block.gpsimd
            def _(e):
                emit(e, by_eng["pool"])

            @block.sync
            def _(e):
                emit(e, by_eng["sp"], last=True)
        st.close()


U_WIN, U_GATE, U_PB, U_PA, U_OUT, U_UP, U_DN = 0, 6, 10, 12, 14, 16, 27
NUNIT = 33


def build_program(n_tiles=5, debug_taps=()):
    nc = bass.Bass("TRN2", target_bir_lowering=False)
    P = Prog(nc)

    def din(name, shape):
        return nc.dram_tensor(name, list(shape), F32, kind="ExternalInput").ap()

    x_d = din("x", [NBLK + 1, 128, D])
    c_d = din("c", [128, 8])
    cmask_d = din("cmask", [128, 1])
    invc_d = din("invc", [128, 64])
    ident_d = din("ident", [128, 128])
    wada_d = din("w_ada", [12, 128, 8, 512])
    bada_d = din("b_ada", [1, 6 * D])
    g1_d = din("g_norm1", [1, D])
    g2_d = din("g_norm2", [1, D])
    wun_d = din("w_units", [U_DN, 128, 8, 512])
    wdn_d = din("w_down", [2, 128, 22, 512])
    lng_d = din("ln_v_g", [D])
    lnb_d = din("ln_v_b", [D])
    gfin_d = din("g_final", [D])
    wsp_d = din("w_spT", [128, 8, 128])
    bsp_d = din("b_sp", [1, D])
    wpool_d = din("w_pool", [128, 4, 2, 256])
    bpool_d = din("b_pool", [128, 8])
    pscale_d = din("pool_scale", [128, 8])
    bgate_d = din("b_gate", [128, 16])
    convw_d = din("conv_w", [128, NCH_UP, 3])
    convb_d = din("conv_b", [128, NCH_UP])
    out_d = nc.dram_tensor("out", [NBLK, 128, D], F32, kind="ExternalOutput").ap()
    wscr = nc.dram_tensor("wscr", [NUNIT, 128, 4096], BF16, kind="Internal").ap()

    ARENA_BYTES = 207 * 1024
    arena = nc.alloc_sbuf_tensor("arena", [128, ARENA_BYTES // 4], F32)
    ps = nc.alloc_psum_tensor("ps", [128, 8, 512], F32)

    def view(off, nbytes, dt=F32, shape=None, rows=None):
        assert off % 4 == 0 and nbytes % 4 == 0 and off + nbytes <= ARENA_BYTES, (off, nbytes)
        a = arena[:, off // 4:(off + nbytes) // 4] if rows is None else arena[0:rows, off // 4:(off + nbytes) // 4]
        if dt == BF16:
            a = a.bitcast(BF16)
        if shape is not None and len(shape) == 2:
            a = a.rearrange("p (a b) -> p a b", a=shape[0])
        elif shape is not None and len(shape) == 3:
            a = a.rearrange("p (a b c) -> p a b c", a=shape[0], b=shape[1])
        return a

    cur = [0]

    def alloc(nbytes, dt=F32, shape=None, rows=None):
        v = view(cur[0], nbytes, dt, shape, rows)
        cur[0] += (nbytes + 31) // 32 * 32
        return v

    ident = alloc(512)
    wmT = alloc(2048, BF16, (8, 128))
    wpool = alloc(4096, BF16, (4, 2, 256))
    lng_b = alloc(4096)
    lnb_b = alloc(4096)
    gfin_b = alloc(4096)
    gt1h_b = alloc(4096)
    gt2_b = alloc(4096)
    modcols = alloc(128)
    bpool = alloc(32)
    pscale = alloc(32)
    hbgate = alloc(64)
    convw = alloc(NCH_UP * 12, F32, (NCH_UP, 3))
    convb = alloc(NCH_UP * 4)
    invc = alloc(256, F32, (4, 16))
    cmask = alloc(32)
    mh = alloc(32)
    ssq = [alloc(32) for _ in range(3)]
    msq = [alloc(32) for _ in range(3)]
    rstd = [alloc(32) for _ in range(3)]
    bnst = alloc(4 * 2 * 6 * 4, F32, (4, 2, 6))
    bnmv = alloc(4 * 2 * 4, F32, (4, 2))
    lnms = alloc(32)
    lnrs = alloc(32)
    lnnm = alloc(32)
    Hh = alloc(NCH_UP * 8, F32, (NCH_UP, 2))
    ph = alloc(NCH_UP * 8, F32, (NCH_UP, 2))
    htmp = alloc(NCH_UP * 4)
    ph0 = alloc(NCH_UP * 8, F32, (NCH_UP, 2))
    hh2 = alloc(32, BF16, (8, 2))
    zpre = alloc(8 * 16 * 4, F32, (8, 16))
    ones_bf = alloc(256, BF16, rows=33)
    ones_f = alloc(512, F32, rows=1)
    bs2 = alloc(2048, BF16, rows=33)
    hT_off = cur[0]
    hT = alloc(8192, BF16, (8, 512))
    bs_f = view(hT_off, 4096, F32, rows=33)
    bs_t = view(hT_off + 4096, 2048, BF16, rows=33)
    ring = [alloc(8192, BF16, (8, 512)) for _ in range(NSLOT)]
    f_off = cur[0]
    f_b = alloc(22528, BF16, (22, 512))
    xb = [alloc(16384, F32, (4, D)) for _ in range(2)]
    base_mix = cur[0]
    RX = base_mix
    zb = view(RX, 16896, F32, (8, 528))
    ptmp_d = [view(RX + 16896, 2112), view(RX + 19008, 2112)]
    pooled = view(RX + 21120, 8192, BF16, (8, 512))
    ya = view(RX, 8192, BF16, (8, 512))
    t2 = view(RX + 8192, 16384, F32, (8, 512))
    t1 = [view(RX + 24576, 2048), view(RX + 26624, 2048)]
    RY = RX + 29312
    vge = [view(RY, 4096), view(RY + 4096, 4096)]
    vg = view(RY + 8192, 8192, BF16, (4, D))
    u_b = view(RY + 16384, 8192, BF16, (8, 512))
    xn = view(RY, 16384, F32, (4, D))
    merged = view(RY + 16384, 8192, BF16, (8, 512))
    RZ = RY + 24576
    yb = view(RZ, 8192, BF16, (8, 512))
    ptmp_p = [view(RZ, 2112), view(RZ + 2112, 2112)]
    th = view(RZ + 8192, 8192, BF16, (8, 512))
    Abuf = [view(RX, 8192, F32, (4, 512)), view(RX + 8192, 8192, F32, (4, 512))]
    Stmp = [view(RX + 16384, 2048), view(RX + 18432, 2048)]
    h2T = view(RX + 20480, 8192, BF16, (8, 512))
    end_mix = RZ + 16384
    assert end_mix <= ARENA_BYTES, end_mix
    xb1_off = base_mix - 16384
    xb0_off = base_mix - 32768
    stg = [view(xb1_off, 16384, F32, (8, 512)), view(f_off, 16384, F32, (8, 512)),
           view(RX, 16384, F32, (8, 512)), view(RY, 16384, F32, (8, 512))]
    NSTG = 4
    acc = [view(f_off + 16384, 2048), view(f_off + 18432, 2048), view(RZ, 2048), view(RZ + 2048, 2048)]
    mrow = [view(f_off + 20480, 2048, rows=1), view(xb0_off + 4096, 2048, rows=1)]
    badar = [view(xb0_off + 6144, 2048, rows=1), view(xb0_off + 8192, 2048, rows=1)]
    gsl = [view(xb0_off + 10240, 2048, rows=1), view(xb0_off + 12288, 2048, rows=1)]
    csb = view(xb0_off + 14336, 32)
    scf = view(xb0_off + 14368, 32)
    onescol = view(xb0_off + 14400, 32)
    wsp_f = view(RZ + 4096, 4096, F32, (8, 128))
    wpool_f = view(RZ + 8192, 8192)

    def isap(a):
        return not isinstance(a, (int, float))

    def DMA(out, in_, key, eng="sp", **kw):
        return P.op(eng, lambda e: e.dma_start(out=out, in_=in_, **kw), reads=[in_], writes=[out], dkey=key)

    def ACT(out, in_, func, scale=1.0, bias=0.0, accum=None):
        rd = [in_] + [a for a in (scale, bias) if isap(a)]
        wr = [out] + ([accum] if accum is not None else [])
        if accum is not None:
            fn = lambda e: e.activation(out=out, in_=in_, func=func, bias=bias, scale=scale, accum_out=accum)
        else:
            fn = lambda e: e.activation(out=out, in_=in_, func=func, bias=bias, scale=scale)
        return P.op("act", fn, rd, wr)

    def TS(eng, out, in0, s1, s2, op0, op1=None):
        rd = [in0] + [a for a in (s1, s2) if a is not None and isap(a)]
        if op1 is None:
            fn = lambda e: e.tensor_scalar(out=out, in0=in0, scalar1=s1, scalar2=None, op0=op0)
        else:
            fn = lambda e: e.tensor_scalar(out=out, in0=in0, scalar1=s1, scalar2=s2, op0=op0, op1=op1)
        return P.op(eng, fn, rd, [out])

    def STT(eng, out, in0, s, in1, op0, op1):
        rd = [in0, in1] + ([s] if isap(s) else [])
        return P.op(eng, lambda e: e.scalar_tensor_tensor(out=out, in0=in0, scalar=s, in1=in1, op0=op0, op1=op1), rd, [out])

    def TT(eng, out, in0, in1, op):
        return P.op(eng, lambda e: e.tensor_tensor(out=out, in0=in0, in1=in1, op=op), [in0, in1], [out])

    def CP(eng, out, in_):
        return P.op(eng, lambda e: e.tensor_copy(out=out, in_=in_), [in_], [out])

    def MS(eng, out, val):
        return P.op(eng, lambda e: e.memset(out, val), [], [out])

    def MM(out, lhsT, rhs, start, stop):
        return P.op("pe", lambda e: e.matmul(out, lhsT=lhsT, rhs=rhs, start=start, stop=stop), [lhsT, rhs], [out])

    def TR(out, in_):
        return P.op("pe", lambda e: e.transpose(out=out, in_=in_, identity=ident), [in_, ident], [out])

    bank = [0]

    def nbank(align=1):
        b = bank[0]
        if align > 1 and b % align:
            b += align - b % align
        b %= 8
        bank[0] = b + 1
        return b

    P.barrier_keys.add("const")

    def unit_src(u):
        if u < U_DN:
            return wun_d[u], 8
        h, kg = divmod(u - U_DN, 3)
        nk = 8 if kg < 2 else 6
        return wdn_d[h, :, kg * 8:kg * 8 + nk, :], nk

    cvt_order = [4, 5, 2, 3, 0, 1, 6, 7, 8, 9, 10, 11, 12, 13, 14, 15] + list(range(U_UP, NUNIT))

    def issue_cast(u):
        src, nk = unit_src(u)
        DMA(wscr[u].rearrange("p (a b) -> p a b", a=8)[:, 0:nk, :], src, "cv%d" % u, eng="pool")

    DMA(csb, c_d, "cvec")
    MS("pool", mh[:, 0:4], -0.5)
    MS("pool", ones_f, 1.0)
    MS("pool", onescol[:, 0:1], 1.0)
    MS("pool", ones_bf, 1.0)
    MS("pool", Hh, 0.0)
    ACT(scf[:, 0:8], csb[:, 0:8], AF.Silu)
    GS1, SHC1, GS2, SHC2 = [modcols[:, i * 8:(i + 1) * 8] for i in range(4)]

    def load_constants():
        DMA(ident, ident_d, "const")
        DMA(cmask[:, 0:1], cmask_d, "const")
        DMA(invc, invc_d.rearrange("p (a b) -> p a b", a=4), "const")
        DMA(lng_b, lng_d.partition_broadcast(128), "const")
        DMA(lnb_b, lnb_d.partition_broadcast(128), "const")
        DMA(gfin_b, gfin_d.partition_broadcast(128), "const")
        DMA(bpool[:, 0:8], bpool_d, "const")
        DMA(pscale[:, 0:8], pscale_d, "const")
        DMA(hbgate[:, 0:16], bgate_d, "const")
        DMA(convw, convw_d, "const")
        DMA(convb[:, 0:NCH_UP], convb_d, "const")
        MS("pool", bs_f, 0.0)
        DMA(bs_f[0:1, :], bsp_d, "bsf0")
        DMA(bs_f[32:33, :], bsp_d, "bsf1")
        DMA(wsp_f, wsp_d, "const")
        DMA(wpool_f[:, 0:2048], wpool_d.rearrange("p a b c -> p (a b c)"), "const")
        CP("dve", wmT, wsp_f)
        MS("pool", wmT[64:128, :, 0:64], 0.0)
        MS("pool", bs2, 0.0)
        CP("dve", bs2[0:1, :], bs_f[0:1, :])
        CP("dve", bs_t[32:33, :], bs_f[32:33, :])
        TT("dve", bs_f[32:33, :], bs_f[32:33, :], bs_t[32:33, :], ALU.subtract)
        CP("dve", bs2[32:33, :], bs_f[32:33, :])
        TS("dve", hbgate[:, 0:16], hbgate[:, 0:16], 0.5, None, ALU.mult)
        CP("dve", wpool.rearrange("p a b c -> p (a b c)"), wpool_f[:, 0:2048])

    wada_key = ["stg%d" % i for i in range(NSTG)]
    mod_bank = {}

    def mod_chain(g):
        s_ = g % NSTG
        kind, half = divmod(g, 2)
        DMA(stg[s_].rearrange("p a b -> p (a b)"), wada_d[g].rearrange("p a b -> p (a b)"), wada_key[s_])
        DMA(badar[g % 2], bada_d[:, g * 512:(g + 1) * 512], "bada%d" % (g % 2))
        if kind in (1, 4):
            DMA(gsl[g % 2], (g1_d if kind == 1 else g2_d)[:, half * 512:(half + 1) * 512], "gsl%d" % (g % 2))
        a = acc[s_]
        NDVE = 6
        for kc in range(NDVE):
            for hh in range(2):
                cs = slice(hh * 256, (hh + 1) * 256)
                if kc == 0:
                    TS("dve", a[:, cs], stg[s_][:, 0, cs], scf[:, 0:1], None, ALU.mult)
                else:
                    STT("dve", a[:, cs], stg[s_][:, kc, cs], scf[:, kc:kc + 1], a[:, cs], ALU.mult, ALU.add)
        b = nbank()
        for kc in range(NDVE, 8):
            MM(ps[0:1, b, :], scf[:, kc:kc + 1], stg[s_][:, kc, :], kc == NDVE, False)
        MM(ps[0:1, b, :], onescol[:, 0:1], a[:, 0:512], False, True)
        mod_bank[g] = b

    def mod_fin(g):
        kind, half = divmod(g, 2)
        b = mod_bank[g]
        row = mrow[g % 2]
        TT("dve", row[:, 0:512], ps[0:1, b, :], badar[g % 2][:, 0:512], ALU.add)
        if kind in (1, 4):
            STT("dve", row[:, 0:512], row[:, 0:512], 1.0, gsl[g % 2][:, 0:512], ALU.add, ALU.mult)
        if kind == 2:
            TS("dve", row[:, 0:512], row[:, 0:512], 0.5, None, ALU.mult)
        if kind in (2, 5):
            b2 = nbank()
            MM(ps[:, b2, :], ones_f[:, 0:128], row[:, 0:512], True, True)
            dst = gt1h_b if kind == 2 else gt2_b
            CP("dve", dst[:, half * 512:(half + 1) * 512], ps[:, b2, :])
        else:
            col0 = {1: 0, 0: 8, 4: 16, 3: 24}[kind] + half * 4
            b2 = nbank()
            for c in range(4):
                MM(ps[:, b2, c:c + 1], row[:, c * 128:(c + 1) * 128], ones_f[:, 0:1], True, True)
            CP("dve", modcols[:, col0:col0 + 4], ps[:, b2, 0:4])

    for g in range(12):
        mod_chain(g)
        if g >= 1:
            mod_fin(g - 1)
    mod_fin(11)
    load_constants()


    MIX_UNITS = [2, 3, 4, 5, 0, 1, 8, 9, 10, 11, 6, 7, 12, 13, 14, 15]
    UP_UNITS = list(range(U_UP, U_DN))
    DN_UNITS = list(range(U_DN, NUNIT))
    seq = []
    for t in range(1, n_tiles):
        seq += [(t, u) for u in MIX_UNITS[:14]]
        if t >= 2:
            seq += [(t - 1, u) for u in DN_UNITS[:3]]
        seq += [(t, u) for u in MIX_UNITS[14:]]
        if t >= 2:
            seq += [(t - 1, u) for u in DN_UNITS[3:]]
        if t >= 1:
            seq += [(t, u) for u in UP_UNITS]
    if n_tiles >= 2:
        seq += [(n_tiles - 1, u) for u in DN_UNITS]
    loaded = [0]
    usepos = [0]

    def ensure_loaded(upto):
        while loaded[0] < min(upto, len(seq)):
            i = loaded[0]
            (tt_, u) = seq[i]
            src, nk = unit_src(u)
            if tt_ <= 1:
                DMA(ring[i % NSLOT].rearrange("p a b -> p (a b)")[:, 0:nk * 512], src.rearrange("p a b -> p (a b)"),
                    "ringc%d" % (i % NSLOT), eng="pool")
            else:
                DMA(ring[i % NSLOT].rearrange("p a b -> p (a b)")[:, 0:nk * 512], wscr[u][:, 0:nk * 512], "ring%d" % (i % NSLOT))
            loaded[0] += 1

    released = [0]

    def next_unit(expect_u):
        i = usepos[0]
        assert seq[i][1] == expect_u, (i, seq[i], expect_u)
        assert i < released[0] + NSLOT
        ensure_loaded(i + 1)
        usepos[0] += 1
        return ring[i % NSLOT]

    def rel(n=1):
        for i in range(released[0], released[0] + n):
            (tt_, u) = seq[i]
            if tt_ == 1 and n_tiles > 2:
                nk = 6 if (u >= U_DN and (u - U_DN) % 3 == 2) else 8
                DMA(wscr[u][:, 0:nk * 512], ring[i % NSLOT].rearrange("p a b -> p (a b)")[:, 0:nk * 512], "wst%d" % (i % NSLOT))
        released[0] += n
        ensure_loaded(released[0] + NSLOT)

    def nb_of(t):
        return 1 if t == 0 else 4

    def stats_xn(xt, nb, k):
        for b in range(nb):
            ACT(xn[:, b, :], xt[:, b, :], AF.Square, accum=ssq[k][:, b:b + 1])
        TS("dve", msq[k][:, 0:nb], ssq[k][:, 0:nb], 1.0 / D, EPS, ALU.mult, ALU.add)
        TT("pool", rstd[k][:, 0:nb], msq[k][:, 0:nb], mh[:, 0:nb], ALU.pow)
        for b in range(nb):
            TS("dve", xn[:, b, :], xt[:, b, :], rstd[k][:, b:b + 1], None, ALU.mult)

    def transposes_to(hdst, nb, gs, shc, last2=False):
        ntok = nb * 128
        for c in range(8):
            bk = nbank()
            for b in range(nb):
                TR(ps[:, bk, b * 128:(b + 1) * 128], xn[:, b, c * 128:(c + 1) * 128])
            if last2:
                ACT(hdst[:, c, :], ps[:, bk, ntok - 2:ntok], AF.Identity, scale=gs[:, c:c + 1], bias=shc[:, c:c + 1])
            else:
                ACT(hdst[:, c, 0:ntok], ps[:, bk, 0:ntok], AF.Identity, scale=gs[:, c:c + 1], bias=shc[:, c:c + 1])

    def fm_group(wslot, j, src, ntok):
        bk = nbank()
        for kc in range(8):
            MM(ps[:, bk, 0:ntok], wslot[:, kc, j * 128:(j + 1) * 128], src[:, kc, 0:ntok], kc == 0, kc == 7)
        return bk

    class _NS:
        pass

    BIG = _NS()
    BIG.zb, BIG.ptmp_d, BIG.ptmp_p, BIG.pooled, BIG.vge, BIG.vg, BIG.u_b = zb, ptmp_d, ptmp_p, pooled, vge, vg, u_b
    BIG.yb, BIG.th, BIG.ya, BIG.t2, BIG.t1, BIG.merged, BIG.hT, BIG.tH = yb, th, ya, t2, t1, merged, hT, t1
    BIG.bnst, BIG.bnmv, BIG.lnms, BIG.lnrs, BIG.lnnm = bnst, bnmv, lnms, lnrs, lnnm
    BIG.get, BIG.rel = next_unit, rel
    B0 = _NS()
    o = f_off
    B0.zb = view(o, 4608, F32, (8, 144)); o += 4608
    B0.ptmp_d = [view(o, 576), view(o + 576, 576)]; o += 1152
    B0.ptmp_p = B0.ptmp_d
    B0.pooled = view(o, 2048, BF16, (8, 128)); o += 2048
    B0.ya = view(o, 2048, BF16, (8, 128)); o += 2048
    B0.t2 = view(o, 4096, F32, (8, 128))
    B0.tH = [view(o, 2048), view(o + 2048, 2048)]; o += 4096
    B0.t1 = [view(o, 512), view(o + 512, 512)]; o += 1024
    B0.vge = [view(o, 4096)]; o += 4096
    B0.vg = view(o, 2048, BF16, (1, D)); o += 2048
    assert o <= f_off + 22528
    o = xb0_off + 4096
    B0.u_b = view(o, 2048, BF16, (8, 128)); o += 2048
    B0.merged = view(o, 2048, BF16, (8, 128)); o += 2048
    B0.yb = view(o, 2048, BF16, (8, 128)); o += 2048
    B0.th = view(o, 2048, BF16, (8, 128)); o += 2048
    B0.hT = view(o, 2048, BF16, (8, 128)); o += 2048
    B0.bnst = view(o, 64, F32, (1, 2, 8))[:, :, :, 0:6]; o += 64
    B0.bnmv = view(o, 32, F32, (1, 8))[:, :, 0:2]; o += 32
    B0.lnms, B0.lnrs, B0.lnnm = view(o, 32), view(o + 32, 32), view(o + 64, 32); o += 96
    assert o <= xb0_off + 16384
    _shared = {}

    def _get0(u):
        _shared[u] = next_unit(u)
        return _shared[u]

    B0.get, B0.rel = _get0, (lambda n=1: None)

    taps = {}

    def tap(name, ap, shape):
        if name in debug_taps:
            d = nc.dram_tensor("dbg_" + name, list(shape), F32 if ap.dtype == F32 else BF16, kind="ExternalOutput").ap()
            taps[name] = P.op("sp", lambda e: e.dma_start(out=d, in_=ap), reads=[ap], writes=[], dkey="tap_" + name)

    out_ops = []

    def load_x(t):
        xt = xb[t % 2]
        if t == 0:
            DMA(xt[:, 0:1, :], x_d[0:1].rearrange("b p d -> p b d"), "x%d" % (t % 2))
        else:
            b0 = 1 + 4 * (t - 1)
            DMA(xt, x_d[b0:b0 + 4].rearrange("b p d -> p b d"), "x%d" % (t % 2))

    def norm1(t):
        nb = nb_of(t)
        stats_xn(xb[t % 2], nb, 0)
        transposes_to(hT, nb, GS1, SHC1)

    def gates(B, first_unit, ntok):
        for q in range(2):
            w = B.get(first_unit + q)
            for j in range(4):
                m = q * 4 + j
                col = (first_unit - U_GATE) * 4 + m
                bk = fm_group(w, j, B.hT, ntok)
                ACT(B.th[:, m, 0:ntok], ps[:, bk, 0:ntok], AF.Tanh, scale=0.5, bias=hbgate[:, col:col + 1])
            B.rel()

    def mixer(t, B):
        zb, ptmp_d, ptmp_p, pooled, vge, vg, u_b, yb, th, ya, t2, t1, merged, hT_ = (
            B.zb, B.ptmp_d, B.ptmp_p, B.pooled, B.vge, B.vg, B.u_b, B.yb, B.th, B.ya, B.t2, B.t1, B.merged, B.hT)
        bnst_, bnmv_, lnms_, lnrs_, lnnm_, tH = B.bnst, B.bnmv, B.lnms, B.lnrs, B.lnnm, B.tH
        halo = (t == 0)
        nb = nb_of(t)
        ntok = nb * 128
        xt = xb[t % 2]
        dbg = (t == 1)
        if dbg:
            tap("hT1", hT_, (128, 8, 512))
        wv = [B.get(2), B.get(3)]
        for b in range(nb):
            vb_ = vge[b % len(vge)]
            for h in range(2):
                bk = nbank()
                for kc in range(8):
                    MM(ps[:, bk, :], hT_[:, kc, b * 128:(b + 1) * 128], wv[h][:, kc, :], kc == 0, kc == 7)
                ACT(vb_[:, h * 512:(h + 1) * 512], ps[:, bk, :], AF.Gelu)
            for h in range(2):
                P.op("dve", lambda e, b=b, h=h, vb_=vb_: e.bn_stats(out=bnst_[:, b, h, :], in_=vb_[:, h * 512:(h + 1) * 512]),
                     [vb_[:, h * 512:(h + 1) * 512]], [bnst_[:, b, h, :]])
            P.op("dve", lambda e, b=b: e.bn_aggr(out=bnmv_[:, b, :], in_=bnst_[:, b, :, :]), [bnst_[:, b, :, :]], [bnmv_[:, b, :]])
            TS("dve", lnms_[:, b:b + 1], bnmv_[:, b, 1:2], EPS, None, ALU.add)
            TT("pool", lnrs_[:, b:b + 1], lnms_[:, b:b + 1], mh[:, 0:1], ALU.pow)
            STT("dve", lnnm_[:, b:b + 1], bnmv_[:, b, 0:1], -1.0, lnrs_[:, b:b + 1], ALU.mult, ALU.mult)
            ACT(vb_[:, 0:D], vb_[:, 0:D], AF.Identity, scale=lnrs_[:, b:b + 1], bias=lnnm_[:, b:b + 1])
            TT("dve", vb_[:, 0:D], vb_[:, 0:D], lng_b[:, 0:D], ALU.mult)
            TT("dve", vg[:, b, :], vb_[:, 0:D], lnb_b[:, 0:D], ALU.add)
        B.rel(2)
        yield
        for half in range(2):
            w = B.get(4 + half)
            for j in range(4):
                m = half * 4 + j
                bk = fm_group(w, j, hT_, ntok)
                ACT(zb[:, m, 16:16 + ntok], ps[:, bk, 0:ntok], AF.Copy)
            B.rel()
        CP("dve", zb[:, :, 0:16], zpre)
        for m in range(8):
            gi = m // 2
            Z = zb[:, m, :]
            L = 16 + ntok
            srcv = Z
            lag = 1
            k = 0
            vs = 0
            ceng = "pool" if (t >= 2 and m % 2 == 1) else "dve"
            ptmp = ptmp_p if ceng == "pool" else ptmp_d
            while lag < WINS[gi]:
                dst = ptmp[k % 2]
                vs2 = vs + lag
                TT(ceng, dst[:, vs2:L], srcv[:, vs2:L], srcv[:, vs2 - lag:L - lag], ALU.add)
                srcv = dst
                vs = vs2
                lag *= 2
                k += 1
            STT("dve", pooled[:, m, 0:ntok], srcv[:, 16:L], 1.0 / WINS[gi], Z[:, 16:L], ALU.mult, ALU.subtract)
            if t == 1:
                TT("dve", ptmp[(k + 1) % 2][:, 0:16], srcv[:, 16:32], invc[:, gi, :], ALU.mult)
                TT("dve", pooled[:, m, 0:16], ptmp[(k + 1) % 2][:, 0:16], Z[:, 16:32], ALU.subtract)
        if dbg:
            tap("pooled", pooled, (128, 8, 512))
        if halo:
            TS("dve", zpre, zb[:, :, ntok:ntok + 16], cmask[:, 0:1], None, ALU.mult)
        else:
            CP("dve", zpre, zb[:, :, ntok:ntok + 16])
        yield
        for half in range(2):
            w = B.get(0 + half)
            for j in range(4):
                m = half * 4 + j
                bk = fm_group(w, j, hT_, ntok)
                ACT(u_b[:, m, 0:ntok], ps[:, bk, 0:ntok], AF.Gelu)
            B.rel()
        yield
        gates(B, U_GATE + 2, ntok)
        yield
        for m in range(8):
            g, dc = divmod(m, 2)
            bk = nbank()
            for cc in range(2):
                MM(ps[:, bk, 0:ntok], wpool[:, g, cc, dc * 128:(dc + 1) * 128], pooled[:, g * 2 + cc, 0:ntok], cc == 0, cc == 1)
            TS("dve", yb[:, m, 0:ntok], ps[:, bk, 0:ntok], bpool[:, m:m + 1], pscale[:, m:m + 1], ALU.add, ALU.mult)
        for g in range(8):
            bk = nbank()
            MM(ps[:, bk, 0:ntok].rearrange("p (a b) -> p a b", a=nb), ones_bf[:, 0:128],
               bs2[:, g * 128:(g + 1) * 128].unsqueeze(1).broadcast_to([33, nb, 128]), True, False)
            for b in range(nb):
                o = ps[:, bk, b * 128:(b + 1) * 128]
                MM(o, vg[:, b, g * 128:(g + 1) * 128], wmT[:, g, :], False, b == nb - 1)
            TT("dve", ya[:, g, 0:ntok], u_b[:, g, 0:ntok], ps[:, bk, 0:ntok], ALU.mult)
        yield
        for half in range(2):
            w = B.get(U_PB + half)
            for j in range(4):
                m = half * 4 + j
                bk = fm_group(w, j, yb, ntok)
                STT("dve", t2[:, m, 0:ntok], th[:, m, 0:ntok], 1.0, ps[:, bk, 0:ntok], ALU.add, ALU.mult)
            B.rel()
        yield
        gates(B, U_GATE, ntok)
        yield
        for half in range(2):
            w = B.get(U_PA + half)
            for j in range(4):
                m = half * 4 + j
                bk = fm_group(w, j, ya, ntok)
                tt = t1[m % 2]
                STT("dve", tt[:, 0:ntok], th[:, m, 0:ntok], 1.0, ps[:, bk, 0:ntok], ALU.add, ALU.mult)
                TT("dve", merged[:, m, 0:ntok], tt[:, 0:ntok], t2[:, m, 0:ntok], ALU.add)
            B.rel()
        if dbg:
            tap("merged", merged, (128, 8, 512))
        yield
        if t >= 2:
            ffn_down_half(t - 1, 0)
        wo = [B.get(U_OUT), B.get(U_OUT + 1)]
        for b in range(nb):
            for h in range(2):
                bk = nbank()
                for kc in range(8):
                    MM(ps[:, bk, :], merged[:, kc, b * 128:(b + 1) * 128], wo[h][:, kc, :], kc == 0, kc == 7)
                xs = xt[:, b, h * 512:(h + 1) * 512]
                tt = tH[(b * 2 + h) % 2]
                TT("dve", tt[:, 0:512], ps[:, bk, :], gt1h_b[:, h * 512:(h + 1) * 512], ALU.mult)
                TT("dve", xs, xs, tt[:, 0:512], ALU.add)
        B.rel(2)
        if dbg:
            tap("xmid", xt, (128, 4, 1024))

    def ffn_down_half(t, h, fin_blocks=False):
        nb = nb_of(t)
        xt = xb[t % 2]
        if True:
            ws = [next_unit(U_DN + h * 3 + kg) for kg in range(3)]
            for b in range(nb):
                bk = nbank()
                for kc in range(22):
                    MM(ps[:, bk, :], f_b[:, kc, b * 128:(b + 1) * 128], ws[kc // 8][:, kc % 8, :], kc == 0, kc == 21)
                xs = xt[:, b, h * 512:(h + 1) * 512]
                tt = dtmp[(h * nb + b) % 2]
                TT("dve", tt[:, 0:512], ps[:, bk, :], gt2_b[:, h * 512:(h + 1) * 512], ALU.mult)
                TT("pool", xs, xs, tt[:, 0:512], ALU.add)
                if fin_blocks:
                    ACT(otmp[:, 0:D], xt[:, b, :], AF.Square, accum=ssq[2][:, b:b + 1])
                    TS("dve", msq[2][:, b:b + 1], ssq[2][:, b:b + 1], 1.0 / D, EPS, ALU.mult, ALU.add)
                    TT("pool", rstd[2][:, b:b + 1], msq[2][:, b:b + 1], mh[:, 0:1], ALU.pow)
                    STT("dve", xt[:, b, :], xt[:, b, :], rstd[2][:, b:b + 1], gfin_b[:, 0:D], ALU.mult, ALU.mult)
                    bo = 4 * (t - 1) + b
                    out_ops.append(DMA(out_d[bo:bo + 1].rearrange("b p d -> p b d"), xt[:, b:b + 1, :], "stl%d" % b))
            rel(3)

    def final_and_store(t):
        nb = nb_of(t)
        xt = xb[t % 2]
        for b in range(nb):
            ACT(otmp[:, 0:D], xt[:, b, :], AF.Square, accum=ssq[2][:, b:b + 1])
        TS("dve", msq[2][:, 0:nb], ssq[2][:, 0:nb], 1.0 / D, EPS, ALU.mult, ALU.add)
        TT("pool", rstd[2][:, 0:nb], msq[2][:, 0:nb], mh[:, 0:nb], ALU.pow)
        for b in range(nb):
            STT("dve", xt[:, b, :], xt[:, b, :], rstd[2][:, b:b + 1], gfin_b[:, 0:D], ALU.mult, ALU.mult)
        b0 = 4 * (t - 1)
        out_ops.append(DMA(out_d[b0:b0 + 4].rearrange("b p d -> p b d"), xt, "st%d" % (t % 2)))

    def ffn_up(t, hook_a=None, hook_b=None):
        halo = (t == 0)
        nb = nb_of(t)
        ntok = nb * 128
        assert not halo
        def halo_contrib(src, r):
            TT("dve", htmp[:, r], src[:, r, 0], convw[:, r, 0], ALU.mult)
            TT("dve", Hh[:, r, 0], src[:, r, 1], convw[:, r, 1], ALU.mult)
            TT("dve", Hh[:, r, 0], Hh[:, r, 0], htmp[:, r], ALU.add)
            TT("dve", Hh[:, r, 1], src[:, r, 1], convw[:, r, 0], ALU.mult)
        if t >= 2:
            halo_contrib(ph, slice(0, NCH_UP))
        for uu in range(11):
            if uu == 2 and hook_a is not None:
                hook_a()
            if uu == 7 and hook_b is not None:
                hook_b()
            w = next_unit(U_UP + uu)
            A = Abuf[uu % 2]
            pend = None
            if t == 1:
                bh = nbank()
                r = slice(uu * 4, uu * 4 + 4)
                for j in range(4):
                    for kc in range(8):
                        MM(ps[:, bh, 2 * j:2 * j + 2], w[:, kc, j * 128:(j + 1) * 128], hh2[:, kc, :], kc == 0, kc == 7)
                TS("dve", ph0[:, r, :], ps[:, bh, 0:8].rearrange("p (a b) -> p a b", a=4), cmask[:, 0:1], None, ALU.mult)
                halo_contrib(ph0, r)
            for j in range(4):
                q = uu * 4 + j
                bk = nbank()
                for kc in range(8):
                    MM(ps[:, bk, 0:ntok], w[:, kc, j * 128:(j + 1) * 128], h2T[:, kc, 0:ntok], kc == 0, kc == 7)
                if not halo:
                    ACT(A[:, j, 0:ntok], ps[:, bk, 0:ntok], AF.Identity, scale=convw[:, q, 2:3], bias=convb[:, q:q + 1])
                ACT(ph[:, q, :], ps[:, bk, ntok - 2:ntok], AF.Copy)
                if not halo:
                    STT("dve", A[:, j, 1:ntok], ps[:, bk, 0:ntok - 1], convw[:, q, 1:2], A[:, j, 1:ntok], ALU.mult, ALU.add)
                    if pend is not None:
                        pend()
                    pend = (lambda j=j, q=q, bk=bk: STT("dve", A[:, j, 2:ntok], ps[:, bk, 0:ntok - 2], convw[:, q, 0:1],
                                                        A[:, j, 2:ntok], ALU.mult, ALU.add))
            if pend is not None:
                pend()
                pend = None
            rel()
            if not halo:
                TT("dve", A[:, :, 0:2], A[:, :, 0:2], Hh[:, uu * 4:(uu + 1) * 4, :], ALU.add)
                for jj in range(2):
                    S = Stmp[jj]
                    ACT(S[:, 0:ntok], A[:, 2 * jj, 0:ntok], AF.Silu)
                    TT("pool", f_b[:, uu * 2 + jj, 0:ntok], S[:, 0:ntok], A[:, 2 * jj + 1, 0:ntok], ALU.mult)

    dtmp = [view(RX + 16384, 2048), view(RX + 18432, 2048)]
    otmp = view(RX, 4096)
    assert n_tiles >= 2
    MS("pool", zpre, 0.0)
    load_x(0)
    load_x(1)
    ensure_loaded(NSLOT)
    stats_xn(xb[0], 1, 0)
    transposes_to(B0.hT, 1, GS1, SHC1)
    stats_xn(xb[1], 4, 0)
    transposes_to(hT, 4, GS1, SHC1)
    BJ = _NS()
    BJ.__dict__.update(BIG.__dict__)
    BJ.get = lambda u: _shared.pop(u)
    g0, g1 = mixer(0, B0), mixer(1, BJ)
    while True:
        d0 = next(g0, "done")
        d1 = next(g1, "done")
        assert (d0 == "done") == (d1 == "done")
        if d0 == "done":
            break
    assert not _shared
    stats_xn(xb[0], 1, 1)
    transposes_to(hh2, 1, GS2, SHC2, last2=True)
    for t in range(1, n_tiles):
        nb = 4
        if t >= 2:
            for _ in mixer(t, BIG):
                pass
        stats_xn(xb[t % 2], nb, 1)
        if t >= 2:
            ffn_down_half(t - 1, 1)
            final_and_store(t - 1)
        if t + 1 < n_tiles:
            load_x(t + 1)
        transposes_to(h2T, nb, GS2, SHC2)
        if t + 1 < n_tiles:
            ha = lambda t=t: stats_xn(xb[(t + 1) % 2], 4, 0)
            hb = lambda t=t: transposes_to(hT, 4, GS1, SHC1)
        else:
            ha = hb = None
        ffn_up(t, hook_a=ha, hook_b=hb)
    if n_tiles >= 2:
        ffn_down_half(n_tiles - 1, 0)
        ffn_down_half(n_tiles - 1, 1, fin_blocks=True)
    assert usepos[0] == len(seq) and released[0] == len(seq), (usepos[0], released[0], len(seq))
    P.finish(out_ops + list(taps.values()))
    P.build()
    return nc, P


def _fm(v, nch):
    return np.ascontiguousarray(np.asarray(v, np.float32).reshape(nch, 128).T)


def _kunits(w):
    K, N = w.shape
    return np.ascontiguousarray(w.reshape(K // 128, 128, N // 512, 512).transpose(2, 1, 0, 3))


def prepare_inputs(x, c, w_ada, b_ada, g_norm1, w_in, ln_v_g, ln_v_b, w_spatial, b_spatial,
                   w_pool, b_pool, pool_scale, w_proj_a, w_proj_b, w_gate, b_gate, w_out,
                   g_norm2, w_up, conv_w, conv_b, w_down, g_final):
    f = lambda a: np.asarray(a, np.float32)
    x, c = f(x), f(c)
    w_up0 = f(w_up)[0]
    perm = np.empty(NCH_UP, np.int64)
    perm[0::2] = np.arange(22)
    perm[1::2] = 22 + np.arange(22)
    colperm = (perm[:, None] * 128 + np.arange(128)[None, :]).reshape(-1)
    w_up_p = w_up0[:, colperm]
    units = np.concatenate([
        _kunits(f(w_in)[0]), _kunits(f(w_gate)[0]), _kunits(f(w_proj_b)[0]), _kunits(f(w_proj_a)[0]),
        _kunits(f(w_out)[0]), _kunits(w_up_p)], axis=0)
    assert units.shape[0] == U_DN
    wdn = np.ascontiguousarray(f(w_down)[0].reshape(22, 128, 2, 512).transpose(2, 1, 0, 3))
    shared = {
        "ident": np.eye(128, dtype=np.float32),
        "w_ada": np.ascontiguousarray(f(w_ada)[0].reshape(8, 128, 12, 512).transpose(2, 1, 0, 3)),
        "b_ada": f(b_ada).reshape(1, 6 * D),
        "g_norm1": f(g_norm1).reshape(1, D),
        "g_norm2": f(g_norm2).reshape(1, D),
        "w_units": units,
        "w_down": wdn,
        "ln_v_g": f(ln_v_g).reshape(D),
        "ln_v_b": f(ln_v_b).reshape(D),
        "g_final": f(g_final).reshape(D),
        "w_spT": np.ascontiguousarray(f(w_spatial)[0].transpose(2, 0, 1)),
        "b_sp": f(b_spatial).reshape(1, D),
        "w_pool": np.ascontiguousarray(f(w_pool)[0].reshape(4, 2, 128, 256).transpose(2, 0, 1, 3)),
        "b_pool": _fm(f(b_pool).reshape(-1), 8),
        "pool_scale": _fm(f(pool_scale).reshape(-1), 8),
        "b_gate": _fm(f(b_gate).reshape(-1), 16),
        "conv_w": np.ascontiguousarray(f(conv_w)[0][:, colperm].reshape(3, NCH_UP, 128).transpose(2, 1, 0)),
        "conv_b": _fm(f(conv_b)[0][colperm], NCH_UP),
    }
    in_maps = []
    for core in range(NCORE):
        b, half = divmod(core, 2)
        s0 = half * TOK
        xc = np.zeros((NBLK + 1, 128, D), np.float32)
        if half:
            xc[0] = x[b, s0 - 128:s0]
        xc[1:] = x[b, s0:s0 + TOK].reshape(NBLK, 128, D)
        invc = np.empty((4, 16), np.float32)
        for gi, w in enumerate(WINS):
            for j in range(16):
                invc[gi, j] = 1.0 / (min(j + 1, w) if half == 0 else w)
        m = dict(shared)
        m["x"] = xc
        m["c"] = _fm(c[b], 8)
        m["cmask"] = np.full((128, 1), float(half), np.float32)
        m["invc"] = np.ascontiguousarray(np.broadcast_to(invc.reshape(1, 64), (128, 64)))
        in_maps.append(m)
    return in_maps


_CACHE = {}


def kernel(**inputs):
    in_maps = prepare_inputs(**inputs)
    if "nc" not in _CACHE:
        _CACHE["nc"] = build_program()[0]
    res = run_bass_kernel_spmd(_CACHE["nc"], in_maps, core_ids=list(range(NCORE)))
    out = np.empty((BATCH, SEQ, D), np.float32)
    for core in range(NCORE):
        b, half = divmod(core, 2)
        out[b, half * TOK:(half + 1) * TOK] = np.asarray(res.results[core]["out"]).reshape(TOK, D)
    return out
```
